# Optimizing a Trainium2 kernel written in Bass

```python
import functools
import jax, jax.numpy as jnp
from jax import lax
import numpy as np


D_MODEL = 1024
BATCH = 4
SEQ = 8192
DEPTH = 2
DEC_BATCH = 128
DEC_SEQ = 8
PAST_LEN = 16384
PAGE_SIZE = 128

HEAD_DIM = 64
N_HEADS = 8
N_KV_HEADS = 2
GROUP = N_HEADS // N_KV_HEADS
WINDOW = 128
BLOCK = 128
ROPE_THETA = 10000.0
Q_W = N_HEADS * HEAD_DIM
KV_W = N_KV_HEADS * HEAD_DIM
MIX_W = Q_W
CONF_W = MIX_W
CONF_K = 31
SC_W = MIX_W
SC_K = 3
N_BRANCH = 3
D_FF = -(-8 * D_MODEL // (3 * 256)) * 256
EPS = 1e-6
_SIZES = (Q_W, KV_W, KV_W, CONF_W, CONF_W, SC_W, SC_W, SC_W, N_BRANCH * D_MODEL)
IN_W = int(sum(_SIZES))
SPLIT_AT = tuple(int(v) for v in np.cumsum(_SIZES)[:-1])

kernel_name = 'hybrid_conformer_shortconv_swa_decoder_step'


def _rmsnorm(x, g):
    x32 = x.astype(jnp.float32)
    y = x32 * lax.rsqrt(jnp.mean(x32 * x32, axis=-1, keepdims=True) + EPS)
    return (y * g.astype(jnp.float32)).astype(x.dtype)


def _layernorm(x, g, b):
    x32 = x.astype(jnp.float32)
    mu = jnp.mean(x32, axis=-1, keepdims=True)
    xc = x32 - mu
    var = jnp.mean(xc * xc, axis=-1, keepdims=True)
    return (xc * lax.rsqrt(var + EPS) * g.astype(jnp.float32) + b.astype(jnp.float32)).astype(x.dtype)


def _rope(x, pos):
    half = HEAD_DIM // 2
    inv = ROPE_THETA ** (-jnp.arange(half, dtype=jnp.float32) / half)
    ang = pos.astype(jnp.float32)[:, None] * inv[None, :]
    cos = jnp.cos(ang)[None, :, None, :]
    sin = jnp.sin(ang)[None, :, None, :]
    x32 = x.astype(jnp.float32)
    x1, x2 = x32[..., :half], x32[..., half:]
    return jnp.concatenate([x1 * cos - x2 * sin, x2 * cos + x1 * sin], axis=-1).astype(x.dtype)


def _causal_dwconv(u, prev, w):
    ext = jnp.concatenate([prev.astype(u.dtype), u], axis=1)
    out = lax.conv_general_dilated(
        ext, w[:, None, :].astype(u.dtype), window_strides=(1,), padding='VALID',
        dimension_numbers=('NWC', 'WIO', 'NWC'), feature_group_count=u.shape[-1])
    return out, ext[:, ext.shape[1] - (w.shape[0] - 1):]


def _sink_softmax(s, mask, sinks):
    sk = sinks.astype(jnp.float32).reshape(N_KV_HEADS, GROUP)[:, :, None, None]
    s = jnp.where(mask, s, -jnp.inf)
    m = jnp.maximum(jnp.max(s, axis=-1, keepdims=True), sk)
    p = jnp.exp(s - m)
    return p / (jnp.sum(p, axis=-1, keepdims=True) + jnp.exp(sk - m))


def _attend_prompt(q, k, v, sinks):
    b, s = q.shape[0], q.shape[1]
    nb = s // BLOCK
    qb = q.reshape(b, nb, BLOCK, N_KV_HEADS, GROUP, HEAD_DIM)
    pad = jnp.zeros((b, BLOCK, N_KV_HEADS, HEAD_DIM), k.dtype)
    kp = jnp.concatenate([pad, k], axis=1).reshape(b, nb + 1, BLOCK, N_KV_HEADS, HEAD_DIM)
    vp = jnp.concatenate([pad, v], axis=1).reshape(b, nb + 1, BLOCK, N_KV_HEADS, HEAD_DIM)
    kb = jnp.concatenate([kp[:, :-1], kp[:, 1:]], axis=2)
    vb = jnp.concatenate([vp[:, :-1], vp[:, 1:]], axis=2)
    sc = jnp.einsum('bnqkgd,bnjkd->bnkgqj', qb, kb,
                    preferred_element_type=jnp.float32) * (HEAD_DIM ** -0.5)
    i = jnp.arange(BLOCK)[:, None]
    j = jnp.arange(2 * BLOCK)[None, :]
    diff = i + BLOCK - j
    key_pos = jnp.arange(nb)[:, None, None] * BLOCK + j[None] - BLOCK
    mask = (diff >= 0)[None] & (diff < WINDOW)[None] & (key_pos >= 0)
    p = _sink_softmax(sc, mask[None, :, None, None], sinks)
    o = jnp.einsum('bnkgqj,bnjkd->bnqkgd', p.astype(v.dtype), vb)
    L = min(WINDOW, s)
    return o.reshape(b, s, Q_W), k[:, s - L:], v[:, s - L:]


def _attend_sample(q, k, v, sinks, k_buf, v_buf):
    b, t = q.shape[0], q.shape[1]
    L = k_buf.shape[1]
    k_all = jnp.concatenate([k_buf.astype(k.dtype), k], axis=1)
    v_all = jnp.concatenate([v_buf.astype(v.dtype), v], axis=1)
    qg = q.reshape(b, t, N_KV_HEADS, GROUP, HEAD_DIM)
    sc = jnp.einsum('btkgd,bjkd->bkgtj', qg, k_all,
                    preferred_element_type=jnp.float32) * (HEAD_DIM ** -0.5)
    q_pos = PAST_LEN + jnp.arange(t)
    k_pos = jnp.concatenate([PAST_LEN - L + jnp.arange(L), q_pos])
    diff = q_pos[:, None] - k_pos[None, :]
    mask = (diff >= 0) & (diff < WINDOW)
    p = _sink_softmax(sc, mask, sinks)
    o = jnp.einsum('bkgtj,bjkd->btkgd', p.astype(v.dtype), v_all)
    n = k_all.shape[1]
    return o.reshape(b, t, Q_W), k_all[:, n - L:], v_all[:, n - L:]


def _layer(x, p, conf_prev, sc_prev, pos, attend):
    (norm1, w_in, sinks, conf_dw_w, conf_dw_b, conf_ln_g, conf_ln_b, sconv_w,
     w_branch, w_out, norm2, w_ffn_in, w_ffn_out) = p
    b, s, _ = x.shape
    h = _rmsnorm(x, norm1)
    z = h @ w_in.astype(h.dtype)
    q, k, v, ca, cg, sb, sc, sx, gl = jnp.split(z, SPLIT_AT, axis=-1)
    q = _rope(q.reshape(b, s, N_HEADS, HEAD_DIM), pos)
    k = _rope(k.reshape(b, s, N_KV_HEADS, HEAD_DIM), pos)
    v = v.reshape(b, s, N_KV_HEADS, HEAD_DIM)
    attn_o, k_state, v_state = attend(q, k, v, sinks)
    u = ca * jax.nn.sigmoid(cg)
    dw, conf_state = _causal_dwconv(u, conf_prev, conf_dw_w)
    conf_o = jax.nn.silu(_layernorm(dw + conf_dw_b.astype(dw.dtype), conf_ln_g, conf_ln_b))
    cu = sc * sx
    conv3, sc_state = _causal_dwconv(cu, sc_prev, sconv_w)
    sc_o = sb * conv3
    gates = jax.nn.sigmoid(gl.reshape(b, s, N_BRANCH, D_MODEL))
    wb = w_branch.astype(h.dtype)
    merged = (gates[:, :, 0] * (attn_o @ wb[0])
              + gates[:, :, 1] * (conf_o @ wb[1])
              + gates[:, :, 2] * (sc_o @ wb[2]))
    x = x + merged @ w_out.astype(h.dtype)
    hn = _rmsnorm(x, norm2)
    gu = hn @ w_ffn_in.astype(hn.dtype)
    g, up = gu[..., :D_FF], gu[..., D_FF:]
    x = x + (jax.nn.silu(g) * up) @ w_ffn_out.astype(hn.dtype)
    return x, k_state, v_state, conf_state, sc_state


def setup_inputs(seed: int = 0) -> dict:
    key = jax.random.key(seed)
    ks = jax.random.split(key, 24)
    f32 = jnp.float32
    L = min(WINDOW, PAST_LEN)
    nrm = lambda k, shape, scale: jax.random.normal(k, shape, f32) * scale
    return {
        'x_prompt': nrm(ks[0], (BATCH, SEQ, D_MODEL), 1.0),
        'x_sample': nrm(ks[1], (DEC_BATCH, DEC_SEQ, D_MODEL), 1.0),
        'cache_k': nrm(ks[2], (DEPTH, DEC_BATCH, L, N_KV_HEADS, HEAD_DIM), 1.0),
        'cache_v': nrm(ks[3], (DEPTH, DEC_BATCH, L, N_KV_HEADS, HEAD_DIM), 1.0),
        'state_conf': nrm(ks[4], (DEPTH, DEC_BATCH, CONF_K - 1, CONF_W), 0.5),
        'state_sconv': nrm(ks[5], (DEPTH, DEC_BATCH, SC_K - 1, SC_W), 0.5),
        'norm1': 1.0 + nrm(ks[6], (DEPTH, D_MODEL), 0.02),
        'w_in': nrm(ks[7], (DEPTH, D_MODEL, IN_W), D_MODEL ** -0.5),
        'sinks': nrm(ks[8], (DEPTH, N_HEADS), 0.5),
        'conf_dw_w': nrm(ks[9], (DEPTH, CONF_K, CONF_W), CONF_K ** -0.5),
        'conf_dw_b': nrm(ks[10], (DEPTH, CONF_W), 0.02),
        'conf_ln_g': 1.0 + nrm(ks[11], (DEPTH, CONF_W), 0.02),
        'conf_ln_b': nrm(ks[12], (DEPTH, CONF_W), 0.02),
        'sconv_w': nrm(ks[13], (DEPTH, SC_K, SC_W), SC_K ** -0.5),
        'w_branch': nrm(ks[14], (DEPTH, N_BRANCH, MIX_W, D_MODEL), MIX_W ** -0.5),
        'w_out': nrm(ks[15], (DEPTH, D_MODEL, D_MODEL), D_MODEL ** -0.5),
        'norm2': 1.0 + nrm(ks[16], (DEPTH, D_MODEL), 0.02),
        'w_ffn_in': nrm(ks[17], (DEPTH, D_MODEL, 2 * D_FF), D_MODEL ** -0.5),
        'w_ffn_out': nrm(ks[18], (DEPTH, D_FF, D_MODEL), D_FF ** -0.5),
        'final_norm': 1.0 + nrm(ks[19], (D_MODEL,), 0.02),
    }


def reference(x_prompt, x_sample, cache_k, cache_v, state_conf, state_sconv,
              norm1, w_in, sinks, conf_dw_w, conf_dw_b, conf_ln_g, conf_ln_b, sconv_w,
              w_branch, w_out, norm2, w_ffn_in, w_ffn_out, final_norm):
    xp, xs = x_prompt, x_sample
    bp, sp = xp.shape[0], xp.shape[1]
    pos_p = jnp.arange(sp)
    pos_s = PAST_LEN + jnp.arange(xs.shape[1])
    kp_l, vp_l, cp_l, sp_l = [], [], [], []
    ks_l, vs_l, cs_l, ss_l = [], [], [], []
    for l in range(DEPTH):
        p = (norm1[l], w_in[l], sinks[l], conf_dw_w[l], conf_dw_b[l], conf_ln_g[l],
             conf_ln_b[l], sconv_w[l], w_branch[l], w_out[l], norm2[l],
             w_ffn_in[l], w_ffn_out[l])
        conf0 = jnp.zeros((bp, CONF_K - 1, CONF_W), xp.dtype)
        sc0 = jnp.zeros((bp, SC_K - 1, SC_W), xp.dtype)
        xp, kp, vp, cp, scp = _layer(xp, p, conf0, sc0, pos_p, _attend_prompt)
        attend_s = functools.partial(_attend_sample, k_buf=cache_k[l], v_buf=cache_v[l])
        xs, kss, vss, css, scs = _layer(xs, p, state_conf[l], state_sconv[l], pos_s, attend_s)
        kp_l.append(kp); vp_l.append(vp); cp_l.append(cp); sp_l.append(scp)
        ks_l.append(kss); vs_l.append(vss); cs_l.append(css); ss_l.append(scs)
    y_prompt = _rmsnorm(xp, final_norm)
    y_sample = _rmsnorm(xs, final_norm)
    return (y_prompt, y_sample,
            jnp.stack(kp_l), jnp.stack(vp_l), jnp.stack(cp_l), jnp.stack(sp_l),
            jnp.stack(ks_l), jnp.stack(vs_l), jnp.stack(cs_l), jnp.stack(ss_l))
```

```python
import contextlib
import os
import numpy as np
import concourse.bass as bass
import concourse.mybir as mybir
from concourse.bass_utils import run_bass_kernel_spmd

F32 = mybir.dt.float32
BF16 = mybir.dt.bfloat16
AF = mybir.ActivationFunctionType
ALU = mybir.AluOpType

D = 1024
NT = 5
GC = NT * 128
NG = 7
NTILE = NT * NG
HALF = ((0, 384), (384, 640))
DFF = 2816
NFF = 22
IN_W = 6400
EPS = 1e-6
NEG = -30000.0
NSLOT = 4
UNIT = 16
NTMP = 10
QB, KB, VB, CAB, CGB, SBB, SCB, SXB, GLB = 0, 512, 640, 768, 1280, 1792, 2304, 2816, 3328
P_G1, P_G2, P_DWW, P_DWB, P_LNG, P_LNB, P_SCW, P_SINK = 0, 8, 16, 140, 144, 148, 152, 164
P_SZ = 168
C_ID = 0
C_FLAG = 128
C_PRM = 129
C_GF = C_PRM + 2 * P_SZ
C_SZ = C_GF + 8
M_P, M_H, M_SN, M_SC = 0, 512, 1024, 1536


def half_of(i):
    return 0 if i < 3 else 1


class Tk:
    __slots__ = ("w", "r", "lo", "hi", "al", "excl")

    def __init__(self, lo=None, hi=None, excl=False):
        self.excl = excl
        self.w = None
        self.r = {}
        self.lo = lo
        self.hi = hi
        self.al = ()


class FW:
    ENG = ("pe", "act", "dve", "pool", "sp")

    def __init__(self, nc):
        self.nc = nc
        self.ops = {e: [] for e in self.ENG}
        self.cnt = {e: 0 for e in self.ENG}
        self.waited = {e: {} for e in self.ENG}
        self.dma_cnt = {}
        self.arena = []

    def arena_tk(self, lo, hi):
        t = Tk(lo, hi)
        self.arena.append(t)
        return t

    def finish_arena(self):
        for t in self.arena:
            t.al = tuple(o for o in self.arena if o is not t and o.lo < t.hi and t.lo < o.hi)

    def op(self, eng, fn, reads=(), writes=(), signal=True, dma=None):
        deps = {}

        def add(k, v):
            if deps.get(k, 0) < v:
                deps[k] = v

        wl = list(writes)
        for t in writes:
            if t.al:
                wl.extend(t.al)
        for t in reads:
            if t.w is not None:
                add(*t.w)
            if t.excl:
                for k, v in t.r.items():
                    if k != eng:
                        add(k, v)
        for t in wl:
            if t.w is not None:
                add(*t.w)
            for k, v in t.r.items():
                add(k, v)
        waits = []
        wd = self.waited[eng]
        for k, v in deps.items():
            if k == eng and eng == "pe":
                continue
            if wd.get(k, 0) < v:
                wd[k] = v
                waits.append((k, v))
        if dma is not None:
            self.dma_cnt[dma] = self.dma_cnt.get(dma, 0) + 16
            rec = (dma, self.dma_cnt[dma])
            inc = (dma, 16)
        elif signal:
            self.cnt[eng] += 1
            rec = (eng, self.cnt[eng])
            inc = (eng, 1)
        else:
            rec = (eng, self.cnt[eng] + 1)
            inc = None
        for t in reads:
            if t.r.get(rec[0], 0) < rec[1]:
                t.r[rec[0]] = rec[1]
        for t in wl:
            t.w = rec
            t.r = {}
        self.ops[eng].append((waits, fn, inc))
        return rec

    def emit(self):
        nc = self.nc
        keys = set(self.ENG) | set(self.dma_cnt.keys())
        with contextlib.ExitStack() as st:
            sems = {}
            for k in sorted(keys):
                sems[k] = st.enter_context(nc.semaphore("s_" + k))
            block = st.enter_context(nc.Block())

            def run(eng_name, extra=None):
                def body(e):
                    for waits, fn, inc in self.ops[eng_name]:
                        for k, v in waits:
                            e.wait_ge(sems[k], v)
                        ins = fn(e)
                        if inc is not None:
                            ins.then_inc(sems[inc[0]], inc[1])
                    if extra:
                        extra(e)
                return body

            def sp_final(e):
                for k, v in self.dma_cnt.items():
                    e.wait_ge(sems[k], v)
                for k in ("pe", "act", "dve", "pool"):
                    if self.cnt[k] > 0:
                        e.wait_ge(sems[k], self.cnt[k])

            block.tensor(run("pe"))
            block.scalar(run("act"))
            block.vector(run("dve"))
            block.gpsimd(run("pool"))
            block.sync(run("sp", sp_final))


def build_program(ng=NG, stop=99):
    nc = bass.Bass("TRN2", target_bir_lowering=False)
    dI = lambda n, s, dt=F32: nc.dram_tensor(n, s, dt, kind="ExternalInput").ap()
    dO = lambda n, s, dt=F32: nc.dram_tensor(n, s, dt, kind="ExternalOutput").ap()
    xin = dI("xin", [NTILE * 128, D])
    NUNIT = 158
    wdr = dI("wts", [int(os.environ.get('DBG_NUNIT', '158')), 128, UNIT * 128])
    cst_d = dI("cst", [128, C_SZ])
    msk_d = dI("msk", [128, 2048])
    rope_d = dI("rope", [2, 128, NTILE * 128])
    ck_d = dI("ck", [2, 16, 128, 128])
    cv_d = dI("cv", [2, 16, 128, 128])
    stc_d = dI("stc", [2, 16 * 30, 512])
    sts_d = dI("sts", [2, 16 * 2, 512])
    yout = dO("yout", [NTILE * 128, D])
    kp_d = dO("kp", [2, 128, 128])
    vp_d = dO("vp", [2, 128, 128])
    cp_d = dO("cp", [2, 30, 512])
    sp_d = dO("sp", [2, 2, 512])
    ks_d = dO("ks", [2, 16, 128, 128])
    vs_d = dO("vs", [2, 16, 128, 128])
    cs_d = dO("cs", [2, 16, 30, 512])
    ss_d = dO("ss", [2, 16, 2, 512])

    fw = FW(nc)
    wspecs = []
    with contextlib.ExitStack() as st:
        def sb(name, shape, dt):
            return st.enter_context(nc.sbuf_tensor("sb_" + name, shape, dt))

        xT = sb("xT", [128, 8, GC], F32)
        tx = [[Tk() for _ in range(2)] for _ in range(8)]
        hT = sb("hT", [128, 8, GC], BF16)
        th = [[Tk() for _ in range(2)] for _ in range(8)]
        mg = sb("mg", [128, 8, GC], BF16)
        tmg = [[Tk() for _ in range(2)] for _ in range(8)]
        UE = 32 + GC
        CE = 4 + GC
        U = sb("U", [128, NFF * GC], BF16)
        OQ, OSB, OCO, OU = 0, 4 * GC, 8 * GC, 12 * GC
        OCU = OU + 4 * UE
        assert OCU + 4 * CE <= NFF * GC
        act_ap = lambda j, a, b: U[:, j * GC + a: j * GC + b]
        tact = [[fw.arena_tk(j * GC + HALF[h][0], j * GC + HALF[h][1]) for h in range(2)] for j in range(NFF)]
        q_ap = lambda c, a, b: U[:, OQ + c * GC + a: OQ + c * GC + b]
        tq = [[fw.arena_tk(OQ + c * GC + i * 128, OQ + c * GC + (i + 1) * 128) for i in range(NT)] for c in range(4)]
        sb_ap = lambda c, a, b: U[:, OSB + c * GC + a: OSB + c * GC + b]
        tsb = [[fw.arena_tk(OSB + c * GC + HALF[h][0], OSB + c * GC + HALF[h][1]) for h in range(2)] for c in range(4)]
        co_ap = lambda c, a, b: U[:, OCO + c * GC + a: OCO + c * GC + b]
        tco = [[fw.arena_tk(OCO + c * GC + HALF[h][0], OCO + c * GC + HALF[h][1]) for h in range(2)] for c in range(4)]
        ue_ap = lambda c, a, b: U[:, OU + c * UE + a: OU + c * UE + b]
        tu = [[fw.arena_tk(OU + c * UE + 32 + HALF[h][0], OU + c * UE + 32 + HALF[h][1]) for h in range(2)] for c in range(4)]
        tuh = [fw.arena_tk(OU + c * UE, OU + c * UE + 32) for c in range(4)]
        ce_ap = lambda c, a, b: U[:, OCU + c * CE + a: OCU + c * CE + b]
        tcu = [[fw.arena_tk(OCU + c * CE + 4 + HALF[h][0], OCU + c * CE + 4 + HALF[h][1]) for h in range(2)] for c in range(4)]
        tcuh = [fw.arena_tk(OCU + c * CE, OCU + c * CE + 4) for c in range(4)]
        fw.finish_arena()

        kTe = [sb(f"kTe{l}", [128, (NT + 1) * 128], BF16) for l in range(2)]
        tkk = [[Tk() for _ in range(NT + 1)] for _ in range(2)]
        vE = [sb(f"vE{l}", [128, NT + 1, 128], BF16) for l in range(2)]
        tvv = [[Tk() for _ in range(NT + 1)] for _ in range(2)]
        kf = sb("kf", [128, GC], F32)
        tkf = [Tk(), Tk()]
        utail = [sb(f"utail{l}", [128, 4, 32], BF16) for l in range(2)]
        tut = [Tk(), Tk()]
        cutail = [sb(f"cutail{l}", [128, 4, 4], BF16) for l in range(2)]
        tct = [Tk(), Tk()]
        ropeT = sb("ropeT", [128, 2, GC], F32)
        trope = Tk()
        cst = sb("cst", [128, C_SZ], F32)
        tcst = Tk()
        msk = sb("msk", [128, 2048], BF16)
        tmsk = Tk()
        ones = sb("ones", [128, 128], BF16)
        tones = Tk()
        esb = sb("esb", [128, 512], F32)
        tesb = Tk()
        esk = sb("esk", [128, 8], F32)
        wsl = [sb(f"wsl{i}", [128, UNIT * 128], BF16) for i in range(NSLOT)]
        twsl = [Tk() for _ in range(NSLOT)]
        stage = [sb(f"stage{i}", [128, D], F32) for i in range(2)]
        tstage = [Tk(), Tk()]
        xstage = [sb(f"xstage{i}", [128, D], F32) for i in range(2)]
        txstage = [Tk(), Tk()]
        spec = sb("spec", [128, 1280], F32)
        tspec = [Tk() for _ in range(4)]
        u32 = sb("u32", [128, 4, 128], F32)
        tu32 = Tk()
        cu32 = sb("cu32", [128, 4, 128], F32)
        tcu32 = Tk()
        tmpb = [sb(f"tmp{i}", [128, 512], F32) for i in range(NTMP)]
        ttmp = [Tk() for _ in range(NTMP)]
        NSQ = 4
        sqb = [sb(f"sq{i}", [128, 384], BF16) for i in range(NSQ)]
        tsq = [Tk() for _ in range(NSQ)]
        identb = sb("identb", [128, 128], BF16)
        tidb = Tk()
        pTb = [sb(f"pT{i}", [128, 512], BF16) for i in range(4)]
        tpT = [Tk() for _ in range(4)]
        dw = sb("dw", [128, 4, GC], F32)
        tdw = [[Tk(), Tk()] for _ in range(4)]
        dwb = sb("dwb", [128, 4, GC], BF16)
        tdwb = [[Tk(), Tk()] for _ in range(4)]
        dw2 = sb("dw2", [128, 4, GC], BF16)
        tdw2 = [[Tk(), Tk()] for _ in range(4)]
        uS = sb("uS", [128, 4, 16, 38], BF16)
        tuS = [Tk() for _ in range(4)]
        cuS = sb("cuS", [128, 4, 16, 10], BF16)
        tcuS = [Tk() for _ in range(4)]
        ckT = sb("ckT", [128, 16, 128], BF16)
        tckT = Tk()
        cvb = sb("cvb", [128, 16, 128], BF16)
        tcvb = Tk()
        ckst = sb("ckst", [128, 4, 128], F32)
        tckst = Tk()

        banks = [st.enter_context(nc.psum_tensor(f"pb{i}", [128, 512], F32)) for i in range(8)]
        tbank = [Tk(excl=True) for _ in range(8)]
        ring = {"pb": 0, "tmp": 0, "sq": 0, "pT": 0, "stage": 0, "dg": 0}
        NRING = 6

        def pb():
            i = ring["pb"]
            ring["pb"] = (i + 1) % NRING
            return banks[i], tbank[i]

        def tmp():
            i = ring["tmp"]
            ring["tmp"] = (i + 1) % NTMP
            return tmpb[i], ttmp[i]

        def rr(name, n):
            i = ring[name]
            ring[name] = (i + 1) % n
            return i

        def MM(out, lhsT, rhs, start, stop, reads, writes, signal, sgc=False):
            fw.op("pe", lambda e: e.matmul(out, lhsT=lhsT, rhs=rhs, start=start, stop=stop, skip_group_check=sgc),
                  reads=reads, writes=writes, signal=signal)

        def TR(out, in_, idn, reads, writes, signal):
            fw.op("pe", lambda e: e.transpose(out=out, in_=in_, identity=idn), reads=reads, writes=writes, signal=signal)

        def ACT(out, in_, func, reads, writes, scale=1.0, bias=0.0):
            fw.op("act", lambda e: e.activation(out=out, in_=in_, func=func, bias=bias, scale=scale),
                  reads=reads, writes=writes)

        def TT(out, in0, in1, op, reads, writes, eng="dve"):
            fw.op(eng, lambda e: e.tensor_tensor(out=out, in0=in0, in1=in1, op=op), reads=reads, writes=writes)

        def STT(out, in0, scalar, in1, op0, op1, reads, writes):
            fw.op("dve", lambda e: e.scalar_tensor_tensor(out=out, in0=in0, scalar=scalar, in1=in1, op0=op0, op1=op1),
                  reads=reads, writes=writes)

        def TS(out, in0, s1, s2, op0, op1, reads, writes, eng="dve"):
            fw.op(eng, lambda e: e.tensor_scalar(out=out, in0=in0, scalar1=s1, scalar2=s2, op0=op0, op1=op1),
                  reads=reads, writes=writes)

        def CP(out, in_, reads, writes, eng="dve"):
            fw.op(eng, lambda e: e.tensor_copy(out=out, in_=in_), reads=reads, writes=writes)

        def RCP(out, in_, reads, writes):
            fw.op("dve", lambda e: e.reciprocal(out=out, in_=in_), reads=reads, writes=writes)

        def DMA(eng, out, in_, reads, writes, key):
            fw.op(eng, lambda e: e.dma_start(out=out, in_=in_), reads=reads, writes=writes, dma=key)

        ident = cst[:, C_ID:C_ID + 128]

        def prm(l, off, n=1):
            o = C_PRM + l * P_SZ + off
            return cst[:, o:o + n]

        wstate = {"cnt": 0, "g": 0}

        def wnext(spec_):
            c = wstate["cnt"]
            if wstate["g"] == 0:
                wspecs.append(spec_)
            u = c // UNIT
            s = u % NSLOT
            if c % UNIT == 0:
                DMA("pool", wsl[s][:], wdr[u], [], [twsl[s]], f"w{s}")
            wstate["cnt"] = c + 1
            o = (c % UNIT) * 128
            return wsl[s][:, o:o + 128], twsl[s]

        def project(l, mat, nk, rows_fn, cols, rhs_fn, rtk_fn):
            wts = [wnext((mat, l, rows_fn(kc), cols)) for kc in range(nk)]
            outs = [pb(), pb()]
            for kc in range(nk):
                for h, (a, b) in enumerate(HALF):
                    bk, tb = outs[h]
                    MM(bk[:, 0:b - a], wts[kc][0], rhs_fn(kc, a, b), kc == 0, kc == nk - 1,
                       [wts[kc][1], rtk_fn(kc, h)], [tb], kc == nk - 1)
            return outs

        nat = lambda kc: np.arange(kc * 128, (kc + 1) * 128)
        hT_rhs = lambda kc, a, b: hT[:, kc, a:b]
        hT_tk = lambda kc, h: th[kc][h]

        pcache = {}

        def proj_h(l, mat, cols):
            key = (l, mat, int(cols[0]), int(cols[-1]))
            if key in pcache:
                return pcache.pop(key)
            return project(l, mat, 8, nat, cols, hT_rhs, hT_tk)

        def preproject(l, mat, cols_list):
            n = len(cols_list)
            wts = {}
            outs = [[pb(), pb()] for _ in range(n)]
            for kc in range(8):
                for j, cols in enumerate(cols_list):
                    wts[(j, kc)] = wnext((mat, l, nat(kc), cols))
                for j in range(n):
                    for h, (a, b) in enumerate(HALF):
                        bk, tb = outs[j][h]
                        MM(bk[:, 0:b - a], wts[(j, kc)][0], hT[:, kc, a:b], kc == 0, kc == 7,
                           [wts[(j, kc)][1], th[kc][h]], [tb], kc == 7)
            assert not pcache
            for j, cols in enumerate(cols_list):
                pcache[(l, mat, int(cols[0]), int(cols[-1]))] = outs[j]

        def tiles_in(h):
            return (0, 1, 2) if h == 0 else (3, 4)

        def load_consts():
            DMA("sp", cst[:], cst_d, [], [tcst], "c0")
            DMA("pool", msk[:], msk_d, [], [tmsk], "c1")
            fw.op("dve", lambda e: e.memset(ones[:], 1.0), writes=[tones])
            CP(identb[:], cst[:, C_ID:C_ID + 128], [tcst], [tidb])

        xpre = {}

        def load_x_dma(g, i):
            s_ = i % 2
            gt = g * NT + i
            DMA("sp", xstage[s_][:], xin[gt * 128:(gt + 1) * 128, :], [], [txstage[s_]], f"xs{s_}")
            xpre[(g, i)] = True

        def prefetch_x(g):
            DMA("sp", ropeT[:], rope_d[:, :, g * GC:(g + 1) * GC].rearrange("t p c -> p t c"), [], [trope], "rope")
            xpre[("rope", g)] = True
            load_x_dma(g, 0)
            load_x_dma(g, 1)

        def load_x(g):
            if ("rope", g) not in xpre:
                DMA("sp", ropeT[:], rope_d[:, :, g * GC:(g + 1) * GC].rearrange("t p c -> p t c"), [], [trope], "rope")
            for i in range(min(2, NT)):
                if (g, i) not in xpre:
                    load_x_dma(g, i)
            for i in range(NT):
                s_ = i % 2
                for bk4 in range(2):
                    bk, tb = pb()
                    for j in range(4):
                        kc = bk4 * 4 + j
                        TR(bk[:, j * 128:(j + 1) * 128], xstage[s_][:, kc * 128:(kc + 1) * 128], ident,
                           [txstage[s_], tcst], [tb], j == 3)
                    eng = "act" if bk4 == 0 else "dve"
                    outap = xT[:, bk4 * 4:(bk4 + 1) * 4, i * 128:(i + 1) * 128]
                    inap = bk[:, 0:512].rearrange("p (a b) -> p a b", b=128)
                    wr = [tx[kc][half_of(i)] for kc in range(bk4 * 4, bk4 * 4 + 4)]
                    if eng == "act":
                        ACT(outap, inap, AF.Copy, [tb], wr)
                    else:
                        CP(outap, inap, [tb], wr)
                if i + 2 < NT:
                    load_x_dma(g, i + 2)

        nst = {"n": 0}

        def norm_chunk(kc):
            first = nst["n"] == 0
            last = nst["n"] == 7
            nst["n"] = (nst["n"] + 1) % 8
            pend = []
            for h, (a, b) in enumerate(HALF):
                w = b - a
                s_ = rr("sq", NSQ)
                ACT(sqb[s_][:, 0:w], xT[:, kc, a:b], AF.Square, [tx[kc][h]], [tsq[s_]])
                pend.append((h, w, s_, first, last))
            return pend

        def norm_mm(pend):
            for h, w, s_, first, last in pend:
                MM(banks[6 + h][:, 0:w], ones[:], sqb[s_][:, 0:w], first, last, [tones, tsq[s_]], [tbank[6 + h]], True)

        def norm_finish(goff_ap, dst_fn, dst_tk):
            assert nst["n"] == 0
            rds = []
            for h, (a, b) in enumerate(HALF):
                w = b - a
                bk, tb = banks[6 + h], tbank[6 + h]
                rs, trs = tmp()
                ACT(rs[:, 0:w], bk[:, 0:w], AF.Sqrt, [tb], [trs], scale=1.0 / D, bias=EPS)
                rds.append((rs, trs))
            for h, (a, b) in enumerate(HALF):
                w = b - a
                rs, trs = rds[h]
                rd, trd = tmp()
                RCP(rd[:, 0:w], rs[:, 0:w], [trs], [trd])
                rds[h] = (rd, trd)
            for kc in range(8):
                for h, (a, b) in enumerate(HALF):
                    w = b - a
                    rd, trd = rds[h]
                    STT(dst_fn(kc, a, b), xT[:, kc, a:b], goff_ap(kc), rd[:, 0:w], ALU.mult, ALU.mult,
                        [tx[kc][h], trd, tcst], [dst_tk(kc, h)])

        def norm(goff_ap, dst_fn, dst_tk):
            for kc in range(8):
                norm_mm(norm_chunk(kc))
            norm_finish(goff_ap, dst_fn, dst_tk)

        class Lag:
            def __init__(self):
                self.p = None

            def push(self, pend):
                if self.p is not None:
                    norm_mm(self.p)
                self.p = pend

            def flush(self):
                if self.p is not None:
                    norm_mm(self.p)
                self.p = None

        def swapcols(base):
            d = np.arange(64)
            sw = np.where(d < 32, d + 32, d - 32)
            return lambda hA, hB: np.concatenate([base + hA * 64 + sw, base + hB * 64 + sw])

        def rope_evac(pq, pqs, dst_fn, wr_fn, dt_reads=()):
            for h, (a, b) in enumerate(HALF):
                w = b - a
                t1, tt1 = tmp()
                TT(t1[:, 0:w], pq[h][0][:, 0:w], ropeT[:, 0, a:b], ALU.mult, [pq[h][1], trope], [tt1])
                t2, tt2 = tmp()
                TT(t2[:, 0:w], pqs[h][0][:, 0:w], ropeT[:, 1, a:b], ALU.mult, [pqs[h][1], trope], [tt2])
                TT(dst_fn(a, b), t1[:, 0:w], t2[:, 0:w], ALU.add, [tt1, tt2], wr_fn(h))

        def special_of(g):
            if g == 0:
                return (0, "S")
            if g == NG - 1:
                return (NT - 1, "L")
            return None

        WSTOP = int(os.environ.get('WSTOP', '99'))

        def win_phase(g, l):
            spc = special_of(g)
            if g > 0:
                for c in range(4):
                    CP(ue_ap(c, 0, 32), utail[l][:, c, :], [tut[l]], [tuh[c]], eng="act" if False else "dve")
                    CP(ce_ap(c, 0, 4), cutail[l][:, c, :], [tct[l]], [tcuh[c]])
            preproject(l, "w_in", [CAB + np.arange(128), CGB + np.arange(128), CAB + 128 + np.arange(128)])
            for c in range(4):
                pa = proj_h(l, "w_in", CAB + c * 128 + np.arange(128))
                pg = proj_h(l, "w_in", CGB + c * 128 + np.arange(128))
                for h, (a, b) in enumerate(HALF):
                    w = b - a
                    sg, tsg = tmp()
                    ACT(sg[:, 0:w], pg[h][0][:, 0:w], AF.Sigmoid, [pg[h][1]], [tsg])
                    TT(ue_ap(c, 32 + a, 32 + b), pa[h][0][:, 0:w], sg[:, 0:w], ALU.mult, [pa[h][1], tsg], [tu[c][h]])
                    if spc is not None and half_of(spc[0]) == h:
                        i = spc[0]
                        o = i * 128 - a
                        TT(u32[:, c, :], pa[h][0][:, o:o + 128], sg[:, o:o + 128], ALU.mult, [pa[h][1], tsg], [tu32])
            for c in range(4):
                p1 = proj_h(l, "w_in", SCB + c * 128 + np.arange(128))
                p2 = proj_h(l, "w_in", SXB + c * 128 + np.arange(128))
                for h, (a, b) in enumerate(HALF):
                    w = b - a
                    t1, tt1 = tmp()
                    ACT(t1[:, 0:w], p1[h][0][:, 0:w], AF.Copy, [p1[h][1]], [tt1])
                    TT(ce_ap(c, 4 + a, 4 + b), p2[h][0][:, 0:w], t1[:, 0:w], ALU.mult, [p2[h][1], tt1], [tcu[c][h]])
                    if spc is not None and half_of(spc[0]) == h:
                        i = spc[0]
                        o = i * 128 - a
                        TT(cu32[:, c, :], p2[h][0][:, o:o + 128], t1[:, o:o + 128], ALU.mult, [p2[h][1], tt1], [tcu32])
            for c in range(4):
                pS = proj_h(l, "w_in", SBB + c * 128 + np.arange(128))
                for h, (a, b) in enumerate(HALF):
                    ACT(sb_ap(c, a, b), pS[h][0][:, 0:b - a], AF.Copy, [pS[h][1]], [tsb[c][h]])
            if spc is not None:
                for src, tsrc, col0, tdst in ((u32, tu32, 256, tspec[2]), (cu32, tcu32, 768, tspec[3])):
                    bk, tb = pb()
                    for c in range(4):
                        TR(bk[:, c * 128:(c + 1) * 128], src[:, c, :], ident, [tsrc, tcst], [tb], c == 3)
                    CP(spec[:, col0:col0 + 512], bk[:, 0:512], [tb], [tdst])
            if g < NG - 1:
                for c in range(4):
                    CP(utail[l][:, c, :], ue_ap(c, GC, GC + 32), [tu[c][1]], [tut[l]])
                    CP(cutail[l][:, c, :], ce_ap(c, GC, GC + 4), [tcu[c][1]], [tct[l]])

        def win_q(g, l):
            for c in range(4):
                cols = np.concatenate([QB + c * 64 + np.arange(64), QB + (4 + c) * 64 + np.arange(64)])
                pq = proj_h(l, "w_in", cols)
                pqs = proj_h(l, "w_in", swapcols(QB)(c, 4 + c))
                rope_evac(pq, pqs, lambda a, b, c=c: q_ap(c, a, b), lambda h, c=c: [tq[c][i] for i in tiles_in(h)])

        def win_phase2(g, l, part):
            spc = special_of(g)
            if part == 0:
                win_q(g, l)
                return
            pk = proj_h(l, "w_in", KB + np.arange(128))
            pks = proj_h(l, "w_in", swapcols(KB)(0, 1))
            rope_evac(pk, pks, lambda a, b: kf[:, a:b], lambda h: [tkf[h]])
            for h, (a, b) in enumerate(HALF):
                ACT(kTe[l][:, 128 + a:128 + b], kf[:, a:b], AF.Copy, [tkf[h]], [tkk[l][1 + i] for i in tiles_in(h)])
            if spc is not None:
                i, kind = spc
                bk, tb = pb()
                TR(bk[:, 0:128], kf[:, i * 128:(i + 1) * 128], ident, [tkf[half_of(i)], tcst], [tb], True)
                ACT(spec[:, 0:128], bk[:, 0:128], AF.Copy, [tb], [tspec[0]])
            wv = [wnext(("w_in", l, nat(kc), VB + np.arange(128))) for kc in range(8)]
            vb2 = [pb(), pb()]
            for i in range(NT):
                bk, tb = vb2[i // 4]
                for kc in range(8):
                    MM(bk[:, (i % 4) * 128:(i % 4 + 1) * 128], hT[:, kc, i * 128:(i + 1) * 128], wv[kc][0], kc == 0, kc == 7,
                       [th[kc][half_of(i)], wv[kc][1]], [tb], kc == 7)
            ACT(vE[l][:, 1:5, :], vb2[0][0][:, 0:512].rearrange("p (a b) -> p a b", b=128), AF.Copy, [vb2[0][1]],
                [tvv[l][1 + i] for i in range(4)])
            ACT(vE[l][:, 5, :], vb2[1][0][:, 0:128], AF.Copy, [vb2[1][1]], [tvv[l][5]])
            if spc is not None:
                i, kind = spc
                bk, tb = vb2[i // 4]
                ACT(spec[:, 128:256], bk[:, (i % 4) * 128:(i % 4 + 1) * 128], AF.Copy, [tb], [tspec[1]])
            if spc is not None:
                special_out(g, l, spc[1])

        def special_out(g, l, kind):
            rall = list(tspec)
            if kind == "L":
                DMA("sp", kp_d[l], spec[:, 0:128], [tspec[0]], [], "o0")
                DMA("sp", vp_d[l], spec[:, 128:256], [tspec[1]], [], "o1")
                DMA("sp", cp_d[l], spec[98:128, 256:768], [tspec[2]], [], "o2")
                DMA("sp", sp_d[l], spec[126:128, 768:1280], [tspec[3]], [], "o3")
            else:
                for b in range(16):
                    DMA("sp", ks_d[l, b, 120:128, :], spec[b * 8:(b + 1) * 8, 0:128], [tspec[0]], [], "o0")
                    DMA("sp", vs_d[l, b, 120:128, :], spec[b * 8:(b + 1) * 8, 128:256], [tspec[1]], [], "o1")
                    DMA("sp", cs_d[l, b, 22:30, :], spec[b * 8:(b + 1) * 8, 256:768], [tspec[2]], [], "o2")
                    DMA("sp", ss_d[l, b, 0:2, :], spec[b * 8 + 6:b * 8 + 8, 768:1280], [tspec[3]], [], "o3")
                DMA("sp", ks_d[l, :, 0:120, :], ck_d[l, :, 8:128, :], [], [], "o4")
                DMA("sp", vs_d[l, :, 0:120, :], cv_d[l, :, 8:128, :], [], [], "o5")
                DMA("sp", cs_d[l, :, 0:22, :], stc_d[l].rearrange("(b t) c -> b t c", t=30)[:, 8:30, :], [], [], "o6")

        def setup_layer(g, l):
            ACT(esk[:, 0:4], prm(l, P_SINK, 4), AF.Exp, [tcst, tesb], [tesb])
            fw.op("dve", lambda e: e.memset(esb[:], 0.0), reads=[], writes=[tesb])
            for c in range(4):
                TS(esb[:, c * 128:(c + 1) * 128], esb[:, c * 128:(c + 1) * 128], esk[:, c:c + 1], None, ALU.add, ALU.bypass,
                   [tesb], [tesb])

        def load_sample_state(l):
            for b4 in range(4):
                DMA("sp", ckst[:], ck_d[l, b4 * 4:(b4 + 1) * 4].rearrange("b j f -> j b f"), [], [tckst], "ckst")
                bk, tb = pb()
                for j in range(4):
                    TR(bk[:, j * 128:(j + 1) * 128], ckst[:, j, :], ident, [tckst, tcst], [tb], j == 3)
                ACT(ckT[:, b4 * 4:(b4 + 1) * 4, :], bk[:, 0:512].rearrange("p (a b) -> p a b", b=128), AF.Copy, [tb], [tckT])
            DMA("pool", cvb[:], cv_d[l].rearrange("b j f -> j b f"), [], [tcvb], "cvb")
            for b4 in range(4):
                s = rr("stage", 2)
                DMA("sp", stage[s][0:120, 0:512], stc_d[l, b4 * 120:(b4 + 1) * 120, :], [], [tstage[s]], f"st{s}")
                bk, tb = pb()
                for c in range(4):
                    TR(bk[:, c * 120:(c + 1) * 120], stage[s][0:120, c * 128:(c + 1) * 128], cst[0:120, C_ID:C_ID + 120],
                       [tstage[s], tcst], [tb], c == 3)
                for c in range(4):
                    ACT(uS[:, c, b4 * 4:(b4 + 1) * 4, 0:30], bk[:, c * 120:(c + 1) * 120].rearrange("p (a b) -> p a b", b=30),
                        AF.Copy, [tb], [tuS[c]])
            s = rr("stage", 2)
            DMA("sp", stage[s][0:32, 0:512], sts_d[l], [], [tstage[s]], f"st{s}")
            bk, tb = pb()
            for c in range(4):
                TR(bk[:, c * 32:(c + 1) * 32], stage[s][0:32, c * 128:(c + 1) * 128], cst[0:32, C_ID:C_ID + 32],
                   [tstage[s], tcst], [tb], c == 3)
            for c in range(4):
                ACT(cuS[:, c, :, 0:2], bk[:, c * 32:(c + 1) * 32].rearrange("p (a b) -> p a b", b=2), AF.Copy, [tb], [tcuS[c]])

        def attn_finish(l, i, ob, tob, sbk, tsbk):
            den, tden = tmp()
            TT(den[:, 0:512], sbk[:, 0:512], esb[:, 0:512], ALU.add, [tsbk, tesb], [tden])
            rden, trden = tmp()
            RCP(rden[:, 0:512], den[:, 0:512], [tden], [trden])
            outap = U[:, OQ + i * 128: OQ + i * 128 + 4 * GC].rearrange("p (c x) -> p c x", x=GC)[:, :, 0:128]
            TT(outap, ob[:, 0:512].rearrange("p (c x) -> p c x", x=128), rden[:, 0:512].rearrange("p (c x) -> p c x", x=128),
               ALU.mult, [tob, trden], [tq[c][i] for c in range(4)])

        def add_mask(bk, tb, mcol):
            MM(bk[:, 0:512], identb[:], msk[:, mcol:mcol + 512], False, True, [tidb, tmsk], [tb], True, sgc=True)

        def score_evac(bk, tb, mcol):
            p = rr("pT", 4)
            ACT(pTb[p][:, 0:512], bk[:, 0:512], AF.Exp, [tb], [tpT[p]], scale=0.125)
            return pTb[p], tpT[p]

        def attn_scores(g, l, i, cp):
            kprev = lambda r0: kTe[l][r0:r0 + 64, i * 128:(i + 1) * 128]
            kcur = lambda r0: kTe[l][r0:r0 + 64, (i + 1) * 128:(i + 2) * 128]
            bks = [pb(), pb()]
            for cc in range(2):
                c = cp * 2 + cc
                for hh in range(2):
                    r0 = hh * 64
                    bk_, tb_ = bks[hh]
                    qa = U[r0:r0 + 64, OQ + c * GC + i * 128: OQ + c * GC + (i + 1) * 128]
                    MM(bk_[:, cc * 256:cc * 256 + 128], kprev(r0), qa, cc == 0, False, [tkk[l][i], tq[c][i]], [tb_], False, sgc=True)
                    MM(bk_[:, cc * 256 + 128:cc * 256 + 256], kcur(r0), qa, False, False, [tkk[l][i + 1], tq[c][i]], [tb_], False, sgc=True)
            mcol = M_H if (g == 0 and i == 3) else M_P
            for hh in range(2):
                add_mask(bks[hh][0], bks[hh][1], mcol)
            return bks

        def attn_evac(g, l, i, cp, bks):
            mcol = M_H if (g == 0 and i == 3) else M_P
            return [score_evac(bks[hh][0], bks[hh][1], mcol) for hh in range(2)]

        def attn_pv(g, l, i, cp, pts):
            ob, tob = pb()
            for cc in range(2):
                for hh in range(2):
                    r0 = hh * 64
                    pT, tp_ = pts[hh]
                    MM(ob[r0:r0 + 64, cc * 128:(cc + 1) * 128], vE[l][:, i, r0:r0 + 64], pT[:, cc * 256:cc * 256 + 128], True, False,
                       [tvv[l][i], tp_], [tob], False)
                    MM(ob[r0:r0 + 64, cc * 128:(cc + 1) * 128], vE[l][:, i + 1, r0:r0 + 64], pT[:, cc * 256 + 128:cc * 256 + 256], False, True,
                       [tvv[l][i + 1], tp_], [tob], False)
                for hh in range(2):
                    r0 = hh * 64
                    pT, tp_ = pts[hh]
                    MM(ob[r0:r0 + 64, 256 + cc * 128:256 + (cc + 1) * 128], ones[:, 0:64], pT[:, cc * 256:cc * 256 + 128], True, False,
                       [tones, tp_], [tob], False)
                    MM(ob[r0:r0 + 64, 256 + cc * 128:256 + (cc + 1) * 128], ones[:, 0:64], pT[:, cc * 256 + 128:cc * 256 + 256], False, True,
                       [tones, tp_], [tob], cc == 1 and hh == 1)
            return ob, tob

        def attn_fin(l, i, cp, ob, tob):
            den, tden = tmp()
            TT(den[:, 0:256], ob[:, 256:512], esb[:, cp * 256:(cp + 1) * 256], ALU.add, [tob, tesb], [tden])
            rden, trden = tmp()
            RCP(rden[:, 0:256], den[:, 0:256], [tden], [trden])
            o0 = OQ + cp * 2 * GC + i * 128
            outap = U[:, o0: o0 + 2 * GC].rearrange("p (c x) -> p c x", x=GC)[:, :, 0:128]
            TT(outap, ob[:, 0:256].rearrange("p (c x) -> p c x", x=128), rden[:, 0:256].rearrange("p (c x) -> p c x", x=128),
               ALU.mult, [tob, trden], [tq[cp * 2][i], tq[cp * 2 + 1][i]])

        def attn_prompt_tiles(g, l, tiles, filler=()):
            steps = [(i, cp) for i in tiles for cp in range(2)]
            n = len(steps)
            filler = list(filler)
            per = -(-len(filler) // max(1, n - 1))
            sc = {0: attn_scores(g, l, *steps[0])}
            ev = {0: attn_evac(g, l, *steps[0], sc[0])}
            for k in range(n):
                i, cp = steps[k]
                if k + 1 < n:
                    sc[k + 1] = attn_scores(g, l, *steps[k + 1])
                    ev[k + 1] = attn_evac(g, l, *steps[k + 1], sc[k + 1])
                ob, tob = attn_pv(g, l, i, cp, ev[k])
                attn_fin(l, i, cp, ob, tob)
                for f_ in filler[k * per:(k + 1) * per]:
                    f_()
            for f_ in filler[n * per:]:
                f_()

        def attn_sample_tile(l):
            ob, tob = pb()
            sbk, tsbk = pb()
            bn = [pb(), pb()]
            bc = [pb(), pb()]
            for c in range(4):
                for hh in range(2):
                    r0 = hh * 64
                    qa = U[r0:r0 + 64, OQ + c * GC: OQ + c * GC + 128]
                    MM(bn[hh][0][:, c * 128:(c + 1) * 128], kTe[l][r0:r0 + 64, 128:256], qa, c == 0, False,
                       [tkk[l][1], tq[c][0]], [bn[hh][1]], False, sgc=True)
                    for b in range(16):
                        MM(bc[hh][0][:, c * 128 + b * 8:c * 128 + b * 8 + 8], ckT[r0:r0 + 64, b, :], qa[:, b * 8:(b + 1) * 8],
                           c == 0 and b == 0, False, [tckT, tq[c][0]], [bc[hh][1]], False, sgc=True)
            for hh in range(2):
                add_mask(bn[hh][0], bn[hh][1], M_SN)
                add_mask(bc[hh][0], bc[hh][1], M_SC)
            pn = [score_evac(bn[hh][0], bn[hh][1], M_SN) for hh in range(2)]
            pc = [score_evac(bc[hh][0], bc[hh][1], M_SC) for hh in range(2)]
            for c in range(4):
                for hh in range(2):
                    r0 = hh * 64
                    MM(ob[r0:r0 + 64, c * 128:(c + 1) * 128], vE[l][:, 1, r0:r0 + 64], pn[hh][0][:, c * 128:(c + 1) * 128], True, False,
                       [tvv[l][1], pn[hh][1]], [tob], False)
                    for b in range(16):
                        MM(ob[r0:r0 + 64, c * 128 + b * 8:c * 128 + b * 8 + 8], cvb[:, b, r0:r0 + 64],
                           pc[hh][0][:, c * 128 + b * 8:c * 128 + b * 8 + 8], False, b == 15, [tcvb, pc[hh][1]], [tob], False)
                for hh in range(2):
                    r0 = hh * 64
                    MM(sbk[r0:r0 + 64, c * 128:(c + 1) * 128], ones[:, 0:64], pn[hh][0][:, c * 128:(c + 1) * 128], True, False,
                       [tones, pn[hh][1]], [tsbk], False)
                    MM(sbk[r0:r0 + 64, c * 128:(c + 1) * 128], ones[:, 0:64], pc[hh][0][:, c * 128:(c + 1) * 128], False, True,
                       [tones, pc[hh][1]], [tsbk], c == 3 and hh == 1)
            attn_finish(l, 0, ob, tob, sbk, tsbk)

        def conv_items(g, l, c, ntap, w_off, ext_ap, hoff, t_hist, t_main, sext, t_sext, evac_fn):
            outs = [(banks[6], tbank[6]), (banks[7], tbank[7])]

            def tap(tau):
                dg_, tdg_ = wnext(("dww" if ntap == 31 else "scw", l, tau, c))
                sh = tau - (ntap - 1) + hoff
                for h, (a, b) in enumerate(HALF):
                    bk, tb = outs[h]
                    a0 = a
                    sgc = False
                    if g == 0 and h == 0:
                        MM(bk[:, 0:128], dg_, sext[:, c, :, tau:tau + 8], tau == 0, tau == ntap - 1,
                           [tdg_, t_sext[c]], [tb], False, sgc=True)
                        a0 = 128
                        sgc = True
                    rds = [tdg_, t_main[c][h]] + ([t_hist[c]] if h == 0 else [t_main[c][0]])
                    MM(bk[:, a0 - a:b - a], dg_, ext_ap(c, a0 + sh, b + sh), (tau == 0) and not sgc, tau == ntap - 1,
                       rds, [tb], (tau == ntap - 1) or h == 1, sgc=sgc)

            items = [(lambda tau=tau: tap(tau)) for tau in range(ntap)]
            items.append(lambda: evac_fn(c, outs))
            return items

        def conf_evac(l, c, outs):
            bias = prm(l, P_DWB + c)
            (a, b) = HALF[0]
            bk, tb = outs[0]
            w = b - a
            fw.op("act", lambda e, bk=bk, w=w, a=a, b=b, c=c, bias=bias: e.activation(
                out=dw[:, c, a:b], in_=bk[:, 0:w], func=AF.Identity, bias=bias, scale=1.0), reads=[tb, tcst], writes=[tdw[c][0]])
            fw.op("act", lambda e, bk=bk, w=w, a=a, b=b, c=c, bias=bias: e.activation(
                out=dwb[:, c, a:b], in_=bk[:, 0:w], func=AF.Identity, bias=bias, scale=1.0), reads=[tb, tcst], writes=[tdwb[c][0]])
            fw.op("act", lambda e, bk=bk, w=w, a=a, b=b, c=c, bias=bias: e.activation(
                out=dw2[:, c, a:b], in_=bk[:, 0:w], func=AF.Square, bias=bias, scale=1.0), reads=[tb, tcst], writes=[tdw2[c][0]])
            (a, b) = HALF[1]
            bk, tb = outs[1]
            w = b - a
            TS(dw[:, c, a:b], bk[:, 0:w], bias, None, ALU.add, ALU.bypass, [tb, tcst], [tdw[c][1]])
            TS(dwb[:, c, a:b], bk[:, 0:w], bias, None, ALU.add, ALU.bypass, [tb, tcst], [tdwb[c][1]])
            TT(dw2[:, c, a:b], dw[:, c, a:b], dw[:, c, a:b], ALU.mult, [tdw[c][1]], [tdw2[c][1]])

        def sconv_evac(l, c, outs):
            for h, (a, b) in enumerate(HALF):
                TT(sb_ap(c, a, b), outs[h][0][:, 0:b - a], sb_ap(c, a, b), ALU.mult, [outs[h][1], tsb[c][h]], [tsb[c][h]])

        def conv_work(g, l):
            if g == 0:
                for c in range(4):
                    CP(uS[:, c, :, 30:38], ue_ap(c, 32, 160).rearrange("p (s t) -> p s t", t=8), [tu[c][0]], [tuS[c]])
                    CP(cuS[:, c, :, 2:10], ce_ap(c, 4, 132).rearrange("p (s t) -> p s t", t=8), [tcu[c][0]], [tcuS[c]])
            items = []
            for c in range(4):
                items += conv_items(g, l, c, 31, P_DWW, ue_ap, 32, tuh, tu, uS, tuS, lambda c_, o_: conf_evac(l, c_, o_))
            items.append(lambda: conf_ln_both(g, l))
            for c in range(4):
                items += conv_items(g, l, c, 3, P_SCW, ce_ap, 4, tcuh, tcu, cuS, tcuS, lambda c_, o_: sconv_evac(l, c_, o_))
            return items

        def conf_ln_both(g, l):
            st_ = []
            for h, (a, b) in enumerate(HALF):
                w = b - a
                if h == 0:
                    bm, tbm, b2, tb2 = banks[6], tbank[6], banks[7], tbank[7]
                else:
                    (bm, tbm), (b2, tb2) = pb(), pb()
                for c in range(4):
                    MM(bm[:, 0:w], ones[:], dwb[:, c, a:b], c == 0, c == 3, [tones, tdwb[c][h]], [tbm], c == 3)
                for c in range(4):
                    MM(b2[:, 0:w], ones[:], dw2[:, c, a:b], c == 0, c == 3, [tones, tdw2[c][h]], [tb2], c == 3)
                st_.append(dict(a=a, b=b, w=w, bm=bm, tbm=tbm, b2=b2, tb2=tb2))
            for d_ in st_:
                d_["mean"], d_["tmean"] = tmp()
                ACT(d_["mean"][:, 0:d_["w"]], d_["bm"][:, 0:d_["w"]], AF.Copy, [d_["tbm"]], [d_["tmean"]], scale=1.0 / 512)
            spare = []
            for d_ in st_:
                w = d_["w"]
                msq, tmsq = tmp()
                spare.append((msq, tmsq))
                TT(msq[:, 0:w], d_["mean"][:, 0:w], d_["mean"][:, 0:w], ALU.mult, [d_["tmean"]], [tmsq])
                d_["var"], d_["tvar"] = tmp()
                STT(d_["var"][:, 0:w], d_["b2"][:, 0:w], 1.0 / 512, msq[:, 0:w], ALU.mult, ALU.subtract, [d_["tb2"], tmsq], [d_["tvar"]])
            for d_ in st_:
                w = d_["w"]
                d_["rs"], d_["trs"] = tmp()
                spare.append((d_["rs"], d_["trs"]))
                ACT(d_["rs"][:, 0:w], d_["var"][:, 0:w], AF.Sqrt, [d_["tvar"]], [d_["trs"]], bias=EPS)
            for d_ in st_:
                w = d_["w"]
                RCP(d_["var"][:, 0:w], d_["rs"][:, 0:w], [d_["trs"]], [d_["tvar"]])
            t1s = {}
            for c in range(4):
                for h, d_ in enumerate(st_):
                    a, b, w = d_["a"], d_["b"], d_["w"]
                    t1, tt1 = spare[(c * 2 + h) % len(spare)]
                    TT(t1[:, 0:w], dw[:, c, a:b], d_["mean"][:, 0:w], ALU.subtract, [tdw[c][h], d_["tmean"]], [tt1])
                    TT(t1[:, 0:w], t1[:, 0:w], d_["var"][:, 0:w], ALU.mult, [tt1, d_["tvar"]], [tt1])
                    fw.op("act", lambda e, c=c, t1=t1, a=a, b=b, w=w: e.activation(
                        out=co_ap(c, a, b), in_=t1[:, 0:w], func=AF.Silu, bias=prm(l, P_LNB + c), scale=prm(l, P_LNG + c)),
                        reads=[tt1, tcst], writes=[tco[c][h]])

        def merge_phase(g, l):
            br_rhs = [
                (lambda kc, a, b: q_ap(kc, a, b), lambda kc, h: None),
                (lambda kc, a, b: co_ap(kc, a, b), lambda kc, h: tco[kc][h]),
                (lambda kc, a, b: sb_ap(kc, a, b), lambda kc, h: tsb[kc][h]),
            ]
            d64 = np.arange(64)
            wb_rows = [
                lambda kc: np.concatenate([kc * 64 + d64, (4 + kc) * 64 + d64]),
                nat, nat,
            ]
            for m in range(8):
                gts = {}
                for br in range(3):
                    pg = proj_h(l, "w_in", GLB + br * 1024 + m * 128 + np.arange(128))
                    gts[br] = []
                    for h, (a, b) in enumerate(HALF):
                        gt_, tgt = tmp()
                        ACT(gt_[:, 0:b - a], pg[h][0][:, 0:b - a], AF.Sigmoid, [pg[h][1]], [tgt])
                        gts[br].append((gt_, tgt))
                acc = [None, None]
                for n_, br in enumerate((0, 2, 1)):
                    wts = [wnext((f"wb{br}", l, wb_rows[br](kc), m * 128 + np.arange(128))) for kc in range(4)]
                    outs = [pb(), pb()]
                    for kc in range(4):
                        for h, (a, b) in enumerate(HALF):
                            if br == 0:
                                rtk = [tq[kc][i] for i in tiles_in(h)]
                            else:
                                rtk = [br_rhs[br][1](kc, h)]
                            MM(outs[h][0][:, 0:b - a], wts[kc][0], br_rhs[br][0](kc, a, b), kc == 0, kc == 3,
                               [wts[kc][1]] + rtk, [outs[h][1]], kc == 3)
                    for h, (a, b) in enumerate(HALF):
                        w = b - a
                        gt_, tgt = gts[br][h]
                        if n_ == 0:
                            t, tt_ = tmp()
                            TT(t[:, 0:w], outs[h][0][:, 0:w], gt_[:, 0:w], ALU.mult, [outs[h][1], tgt], [tt_])
                            acc[h] = (t, tt_)
                        else:
                            t2, tt2 = tmp()
                            TT(t2[:, 0:w], outs[h][0][:, 0:w], gt_[:, 0:w], ALU.mult, [outs[h][1], tgt], [tt2])
                            t, tt_ = acc[h]
                            if n_ == 1:
                                TT(t[:, 0:w], t[:, 0:w], t2[:, 0:w], ALU.add, [tt_, tt2], [tt_])
                            else:
                                TT(mg[:, m, a:b], t[:, 0:w], t2[:, 0:w], ALU.add, [tt_, tt2], [tmg[m][h]])
            lag = Lag()
            for m in range(8):
                po = project(l, "w_out", 8, nat, m * 128 + np.arange(128), lambda kc, a, b: mg[:, kc, a:b], lambda kc, h: tmg[kc][h])
                for h, (a, b) in enumerate(HALF):
                    TT(xT[:, m, a:b], po[h][0][:, 0:b - a], xT[:, m, a:b], ALU.add, [po[h][1], tx[m][h]], [tx[m][h]])
                lag.push(norm_chunk(m))
            lag.flush()

        def ffn_phase(g, l):
            preproject(l, "w_ffn_in", [np.arange(128), DFF + np.arange(128), 128 + np.arange(128)])
            for j in range(NFF):
                pgt = proj_h(l, "w_ffn_in", j * 128 + np.arange(128))
                pup = proj_h(l, "w_ffn_in", DFF + j * 128 + np.arange(128))
                for h, (a, b) in enumerate(HALF):
                    w = b - a
                    sg, tsg = tmp()
                    ACT(sg[:, 0:w], pgt[h][0][:, 0:w], AF.Silu, [pgt[h][1]], [tsg])
                    TT(act_ap(j, a, b), pup[h][0][:, 0:w], sg[:, 0:w], ALU.mult, [pup[h][1], tsg], [tact[j][h]])
            lag = Lag()
            for m in range(8):
                po = project(l, "w_ffn_out", NFF, nat, m * 128 + np.arange(128), lambda kc, a, b: act_ap(kc, a, b),
                             lambda kc, h: tact[kc][h])
                for h, (a, b) in enumerate(HALF):
                    TT(xT[:, m, a:b], po[h][0][:, 0:b - a], xT[:, m, a:b], ALU.add, [po[h][1], tx[m][h]], [tx[m][h]])
                lag.push(norm_chunk(m))
            lag.flush()

        def carry_kv(g, l):
            if g < NG - 1:
                CP(kTe[l][:, 0:128], kTe[l][:, NT * 128:(NT + 1) * 128], [tkk[l][NT]], [tkk[l][0]])
                CP(vE[l][:, 0, :], vE[l][:, NT, :], [tvv[l][NT]], [tvv[l][0]])

        def final_out(g):
            norm_finish(lambda kc: cst[:, C_GF + kc:C_GF + kc + 1], lambda kc, a, b: xT[:, kc, a:b], lambda kc, h: tx[kc][h])
            for i in range(NT):
                gt = g * NT + i
                s = rr("stage", 2)
                for bk4 in range(2):
                    bk, tb = pb()
                    for j in range(4):
                        kc = bk4 * 4 + j
                        TR(bk[:, j * 128:(j + 1) * 128], xT[:, kc, i * 128:(i + 1) * 128], ident,
                           [tx[kc][half_of(i)], tcst], [tb], j == 3)
                    if bk4 == 0:
                        ACT(stage[s][:, 0:512], bk[:, 0:512], AF.Copy, [tb], [tstage[s]])
                    else:
                        CP(stage[s][:, 512:1024], bk[:, 0:512], [tb], [tstage[s]])
                DMA("sp", yout[gt * 128:(gt + 1) * 128, :], stage[s][:], [tstage[s]], [], f"st{s}")

        load_consts()
        for g in range(ng):
            wstate["g"] = g
            wstate["cnt"] = 0
            load_x(g)
            for l in range(2):
                setup_layer(g, l)
                if g == 0:
                    load_sample_state(l)
                n1 = (lambda kc, l=l: prm(l, P_G1 + kc), lambda kc, a, b: hT[:, kc, a:b], lambda kc, h: th[kc][h])
                if l == 0:
                    norm(*n1)
                else:
                    norm_finish(*n1)
                win_phase(g, l)
                win_phase2(g, l, 0)
                win_phase2(g, l, 1)
                cw = conv_work(g, l)
                if g == 0:
                    attn_sample_tile(l)
                    attn_prompt_tiles(g, l, range(1, NT), cw)
                else:
                    attn_prompt_tiles(g, l, range(NT), cw)
                carry_kv(g, l)
                merge_phase(g, l)
                if l == 1 and g + 1 < ng:
                    prefetch_x(g + 1)
                norm_finish(lambda kc, l=l: prm(l, P_G2 + kc), lambda kc, a, b: hT[:, kc, a:b], lambda kc, h: th[kc][h])
                ffn_phase(g, l)
                if g == 0 and l == 0:
                    TS(xT[:, :, 256:384], xT[:, :, 256:384], cst[:, C_FLAG:C_FLAG + 1], None, ALU.mult, ALU.bypass,
                       [tx[kc][0] for kc in range(8)] + [tcst], [tx[kc][0] for kc in range(8)])
            assert stop < 99 or wstate["cnt"] == NUNIT * UNIT, wstate["cnt"]
            final_out(g)
        fw.emit()
    return nc, wspecs


_CACHE = {}


def _get_program():
    if "p" not in _CACHE:
        _CACHE["p"] = build_program()
    return _CACHE["p"]


def _pack_weights(wspecs, mats):
    n = len(wspecs)
    out = np.zeros((158, 128, UNIT * 128), np.float32)
    for t, (mat, l, rows, cols) in enumerate(wspecs):
        W = mats[mat][l]
        if mat in ("dww", "scw"):
            tau, c = rows, cols
            blk = np.zeros((128, 128), np.float32)
            blk[np.arange(128), np.arange(128)] = W[tau, c * 128:(c + 1) * 128]
            out[t // UNIT, :, (t % UNIT) * 128:(t % UNIT + 1) * 128] = blk
            continue
        c0 = int(cols[0])
        if np.array_equal(cols, np.arange(c0, c0 + 128)):
            blk = W[:, c0:c0 + 128]
        else:
            blk = W[:, cols]
        r0 = int(rows[0])
        if np.array_equal(rows, np.arange(r0, r0 + 128)):
            blk = blk[r0:r0 + 128]
        else:
            blk = blk[rows]
        out[t // UNIT, :, (t % UNIT) * 128:(t % UNIT + 1) * 128] = blk
    return out


def _masks(first_half):
    j = np.arange(128)[:, None]
    q = np.arange(128)[None, :]
    prev = np.where(j > q, 0.0, NEG)
    cur = np.where(j <= q, 0.0, NEG)
    mP = np.concatenate([prev, cur, prev, cur], axis=1)
    pv = np.full((128, 128), NEG) if first_half else prev
    mH = np.concatenate([pv, cur, pv, cur], axis=1)
    sn = np.where(((j // 8) == (q // 8)) & ((j % 8) <= (q % 8)), 0.0, NEG)
    mSn = np.concatenate([sn] * 4, axis=1)
    t = np.arange(512)[None, :] % 8
    mSc = np.where(j >= t + 1, 0.0, NEG)
    return np.concatenate([mP, mH, mSn, mSc], axis=1).astype(np.float32)


def _rope_tables(pos):
    half = 32
    inv = (np.float32(10000.0) ** (-(np.arange(half, dtype=np.float32) / np.float32(half)))).astype(np.float32)
    ang = (pos.astype(np.float32)[None, :] * inv[:, None]).astype(np.float32)
    cs = np.cos(ang.astype(np.float64)).astype(np.float32)
    sn = np.sin(ang.astype(np.float64)).astype(np.float32)
    d = np.arange(128) % 64
    cosT = cs[d % 32]
    sinT = sn[d % 32] * np.where(d >= 32, 1.0, -1.0).astype(np.float32)[:, None]
    return np.stack([cosT, sinT]).astype(np.float32)


def kernel(x_prompt, x_sample, cache_k, cache_v, state_conf, state_sconv,
           norm1, w_in, sinks, conf_dw_w, conf_dw_b, conf_ln_g, conf_ln_b, sconv_w,
           w_branch, w_out, norm2, w_ffn_in, w_ffn_out, final_norm):
    f = lambda a: np.ascontiguousarray(np.asarray(a, dtype=np.float32))
    x_prompt, x_sample, cache_k, cache_v = f(x_prompt), f(x_sample), f(cache_k), f(cache_v)
    state_conf, state_sconv = f(state_conf), f(state_sconv)
    nc, wspecs = _get_program()
    w_branch = f(w_branch)
    mats = {"w_in": f(w_in), "wb0": w_branch[:, 0], "wb1": w_branch[:, 1], "wb2": w_branch[:, 2],
            "w_out": f(w_out), "w_ffn_in": f(w_ffn_in), "w_ffn_out": f(w_ffn_out),
            "dww": f(conf_dw_w), "scw": f(sconv_w)}
    wts = _pack_weights(wspecs, mats)
    cstb = np.zeros((128, C_SZ), np.float32)
    cstb[:, C_ID:C_ID + 128] = np.eye(128, dtype=np.float32)
    p = np.arange(128)
    for l in range(2):
        o = C_PRM + l * P_SZ
        cstb[:, o + P_G1:o + P_G1 + 8] = f(norm1)[l].reshape(8, 128).T
        cstb[:, o + P_G2:o + P_G2 + 8] = f(norm2)[l].reshape(8, 128).T
        dww = f(conf_dw_w)[l]
        cstb[:, o + P_DWW:o + P_DWW + 124] = dww.reshape(31, 4, 128).transpose(2, 1, 0).reshape(128, 124)
        cstb[:, o + P_DWB:o + P_DWB + 4] = f(conf_dw_b)[l].reshape(4, 128).T
        cstb[:, o + P_LNG:o + P_LNG + 4] = f(conf_ln_g)[l].reshape(4, 128).T
        cstb[:, o + P_LNB:o + P_LNB + 4] = f(conf_ln_b)[l].reshape(4, 128).T
        scw = f(sconv_w)[l]
        cstb[:, o + P_SCW:o + P_SCW + 12] = scw.reshape(3, 4, 128).transpose(2, 1, 0).reshape(128, 12)
        sk = f(sinks)[l]
        for c in range(4):
            cstb[:, o + P_SINK + c] = np.where(p < 64, sk[c], sk[4 + c])
    cstb[:, C_GF:C_GF + 8] = f(final_norm).reshape(8, 128).T

    in_maps = []
    for core in range(8):
        seq, hf = core // 2, core % 2
        start = hf * 4096
        xs = x_sample[core * 16:(core + 1) * 16].reshape(128, D)
        if hf == 0:
            halo = np.zeros((256, D), np.float32)
        else:
            halo = x_prompt[seq, start - 256:start]
        xin = np.concatenate([xs, halo, x_prompt[seq, start:start + 4096]], axis=0)
        pos = np.concatenate([16384 + (np.arange(128) % 8), np.maximum(start - 256 + np.arange(256), 0),
                              start + np.arange(4096)])
        cc = cstb.copy()
        cc[:, C_FLAG] = float(hf)
        in_maps.append({
            "xin": np.ascontiguousarray(xin),
            "wts": wts,
            "cst": cc,
            "msk": _masks(hf == 0),
            "rope": _rope_tables(pos),
            "ck": np.ascontiguousarray(cache_k[:, core * 16:(core + 1) * 16].reshape(2, 16, 128, 128)),
            "cv": np.ascontiguousarray(cache_v[:, core * 16:(core + 1) * 16].reshape(2, 16, 128, 128)),
            "stc": np.ascontiguousarray(state_conf[:, core * 16:(core + 1) * 16].reshape(2, 480, 512)),
            "sts": np.ascontiguousarray(state_sconv[:, core * 16:(core + 1) * 16].reshape(2, 32, 512)),
        })
    ncore = int(os.environ.get('DBG_CORES', '8'))
    if ncore < 8:
        nu = int(os.environ.get('DBG_NUNIT', '158'))
        for m in in_maps:
            m['wts'] = m['wts'][:nu]
    res = run_bass_kernel_spmd(nc, in_maps[:ncore], core_ids=list(range(ncore)))
    R = list(res.results) + [res.results[0]] * (8 - ncore)
    y_prompt = np.empty((4, 8192, D), np.float32)
    y_sample = np.empty((128, 8, D), np.float32)
    k_prompt = np.empty((2, 4, 128, 2, 64), np.float32)
    v_prompt = np.empty((2, 4, 128, 2, 64), np.float32)
    conf_prompt = np.empty((2, 4, 30, 512), np.float32)
    sconv_prompt = np.empty((2, 4, 2, 512), np.float32)
    k_sample = np.empty((2, 128, 128, 2, 64), np.float32)
    v_sample = np.empty((2, 128, 128, 2, 64), np.float32)
    conf_sample = np.empty((2, 128, 30, 512), np.float32)
    sconv_sample = np.empty((2, 128, 2, 512), np.float32)
    for core in range(8):
        seq, hf = core // 2, core % 2
        r = R[core]
        yo = r["yout"]
        y_sample[core * 16:(core + 1) * 16] = yo[0:128].reshape(16, 8, D)
        y_prompt[seq, hf * 4096:(hf + 1) * 4096] = yo[384:]
        if hf == 1:
            k_prompt[:, seq] = r["kp"].reshape(2, 128, 2, 64)
            v_prompt[:, seq] = r["vp"].reshape(2, 128, 2, 64)
            conf_prompt[:, seq] = r["cp"]
            sconv_prompt[:, seq] = r["sp"]
        k_sample[:, core * 16:(core + 1) * 16] = r["ks"].reshape(2, 16, 128, 2, 64)
        v_sample[:, core * 16:(core + 1) * 16] = r["vs"].reshape(2, 16, 128, 2, 64)
        conf_sample[:, core * 16:(core + 1) * 16] = r["cs"]
        sconv_sample[:, core * 16:(core + 1) * 16] = r["ss"]
    return (y_prompt, y_sample, k_prompt, v_prompt, conf_prompt, sconv_prompt,
            k_sample, v_sample, conf_sample, sconv_sample)
```

```python
import contextlib
import os
import numpy as np
import concourse.bass as bass
import concourse.mybir as mybir
from concourse.bass_utils import run_bass_kernel_spmd

F32 = mybir.dt.float32
BF16 = mybir.dt.bfloat16
AF = mybir.ActivationFunctionType
ALU = mybir.AluOpType

D = 1024
NT = 5
GC = NT * 128
NG = 7
NTILE = NT * NG
HALF = ((0, 384), (384, 640))
DFF = 2816
NFF = 22
IN_W = 6400
EPS = 1e-6
NEG = -30000.0
NSLOT = 4
UNIT = 16
NTMP = 10
QB, KB, VB, CAB, CGB, SBB, SCB, SXB, GLB = 0, 512, 640, 768, 1280, 1792, 2304, 2816, 3328
P_G1, P_G2, P_DWW, P_DWB, P_LNG, P_LNB, P_SCW, P_SINK = 0, 8, 16, 140, 144, 148, 152, 164
P_SZ = 168
C_ID = 0
C_FLAG = 128
C_PRM = 129
C_GF = C_PRM + 2 * P_SZ
C_SZ = C_GF + 8
M_P, M_H, M_SN, M_SC = 0, 512, 1024, 1536


def half_of(i):
    return 0 if i < 3 else 1


class Tk:
    __slots__ = ("w", "r", "lo", "hi", "al", "excl")

    def __init__(self, lo=None, hi=None, excl=False):
        self.excl = excl
        self.w = None
        self.r = {}
        self.lo = lo
        self.hi = hi
        self.al = ()


class FW:
    ENG = ("pe", "act", "dve", "pool", "sp")

    def __init__(self, nc):
        self.nc = nc
        self.ops = {e: [] for e in self.ENG}
        self.cnt = {e: 0 for e in self.ENG}
        self.waited = {e: {} for e in self.ENG}
        self.dma_cnt = {}
        self.arena = []

    def arena_tk(self, lo, hi):
        t = Tk(lo, hi)
        self.arena.append(t)
        return t

    def finish_arena(self):
        for t in self.arena:
            t.al = tuple(o for o in self.arena if o is not t and o.lo < t.hi and t.lo < o.hi)

    def op(self, eng, fn, reads=(), writes=(), signal=True, dma=None):
        deps = {}

        def add(k, v):
            if deps.get(k, 0) < v:
                deps[k] = v

        wl = list(writes)
        for t in writes:
            if t.al:
                wl.extend(t.al)
        for t in reads:
            if t.w is not None:
                add(*t.w)
            if t.excl:
                for k, v in t.r.items():
                    if k != eng:
                        add(k, v)
        for t in wl:
            if t.w is not None:
                add(*t.w)
            for k, v in t.r.items():
                add(k, v)
        waits = []
        wd = self.waited[eng]
        for k, v in deps.items():
            if k == eng and eng == "pe":
                continue
            if wd.get(k, 0) < v:
                wd[k] = v
                waits.append((k, v))
        if dma is not None:
            self.dma_cnt[dma] = self.dma_cnt.get(dma, 0) + 16
            rec = (dma, self.dma_cnt[dma])
            inc = (dma, 16)
        elif signal:
            self.cnt[eng] += 1
            rec = (eng, self.cnt[eng])
            inc = (eng, 1)
        else:
            rec = (eng, self.cnt[eng] + 1)
            inc = None
        for t in reads:
            if t.r.get(rec[0], 0) < rec[1]:
                t.r[rec[0]] = rec[1]
        for t in wl:
            t.w = rec
            t.r = {}
        self.ops[eng].append((waits, fn, inc))
        return rec

    def emit(self):
        nc = self.nc
        keys = set(self.ENG) | set(self.dma_cnt.keys())
        with contextlib.ExitStack() as st:
            sems = {}
            for k in sorted(keys):
                sems[k] = st.enter_context(nc.semaphore("s_" + k))
            block = st.enter_context(nc.Block())

            def run(eng_name, extra=None):
                def body(e):
                    for waits, fn, inc in self.ops[eng_name]:
                        for k, v in waits:
                            e.wait_ge(sems[k], v)
                        ins = fn(e)
                        if inc is not None:
                            ins.then_inc(sems[inc[0]], inc[1])
                    if extra:
                        extra(e)
                return body

            def sp_final(e):
                for k, v in self.dma_cnt.items():
                    e.wait_ge(sems[k], v)
                for k in ("pe", "act", "dve", "pool"):
                    if self.cnt[k] > 0:
                        e.wait_ge(sems[k], self.cnt[k])

            block.tensor(run("pe"))
            block.scalar(run("act"))
            block.vector(run("dve"))
            block.gpsimd(run("pool"))
            block.sync(run("sp", sp_final))


def build_program(ng=NG, stop=99):
    nc = bass.Bass("TRN2", target_bir_lowering=False)
    dI = lambda n, s, dt=F32: nc.dram_tensor(n, s, dt, kind="ExternalInput").ap()
    dO = lambda n, s, dt=F32: nc.dram_tensor(n, s, dt, kind="ExternalOutput").ap()
    xin = dI("xin", [NTILE * 128, D])
    NUNIT = 158
    wdr = dI("wts", [int(os.environ.get('DBG_NUNIT', '158')), 128, UNIT * 128])
    cst_d = dI("cst", [128, C_SZ])
    msk_d = dI("msk", [128, 2048])
    rope_d = dI("rope", [2, 128, NTILE * 128])
    ck_d = dI("ck", [2, 16, 128, 128])
    cv_d = dI("cv", [2, 16, 128, 128])
    stc_d = dI("stc", [2, 16 * 30, 512])
    sts_d = dI("sts", [2, 16 * 2, 512])
    yout = dO("yout", [NTILE * 128, D])
    kp_d = dO("kp", [2, 128, 128])
    vp_d = dO("vp", [2, 128, 128])
    cp_d = dO("cp", [2, 30, 512])
    sp_d = dO("sp", [2, 2, 512])
    ks_d = dO("ks", [2, 16, 128, 128])
    vs_d = dO("vs", [2, 16, 128, 128])
    cs_d = dO("cs", [2, 16, 30, 512])
    ss_d = dO("ss", [2, 16, 2, 512])

    fw = FW(nc)
    wspecs = []
    with contextlib.ExitStack() as st:
        def sb(name, shape, dt):
            return st.enter_context(nc.sbuf_tensor("sb_" + name, shape, dt))

        xT = sb("xT", [128, 8, GC], F32)
        tx = [[Tk() for _ in range(2)] for _ in range(8)]
        hT = sb("hT", [128, 8, GC], BF16)
        th = [[Tk() for _ in range(2)] for _ in range(8)]
        mg = sb("mg", [128, 8, GC], BF16)
        tmg = [[Tk() for _ in range(2)] for _ in range(8)]
        UE = 32 + GC
        CE = 4 + GC
        U = sb("U", [128, NFF * GC], BF16)
        OQ, OSB, OCO, OU = 0, 4 * GC, 8 * GC, 12 * GC
        OCU = OU + 4 * UE
        assert OCU + 4 * CE <= NFF * GC
        act_ap = lambda j, a, b: U[:, j * GC + a: j * GC + b]
        tact = [[fw.arena_tk(j * GC + HALF[h][0], j * GC + HALF[h][1]) for h in range(2)] for j in range(NFF)]
        q_ap = lambda c, a, b: U[:, OQ + c * GC + a: OQ + c * GC + b]
        tq = [[fw.arena_tk(OQ + c * GC + i * 128, OQ + c * GC + (i + 1) * 128) for i in range(NT)] for c in range(4)]
        sb_ap = lambda c, a, b: U[:, OSB + c * GC + a: OSB + c * GC + b]
        tsb = [[fw.arena_tk(OSB + c * GC + HALF[h][0], OSB + c * GC + HALF[h][1]) for h in range(2)] for c in range(4)]
        co_ap = lambda c, a, b: U[:, OCO + c * GC + a: OCO + c * GC + b]
        tco = [[fw.arena_tk(OCO + c * GC + HALF[h][0], OCO + c * GC + HALF[h][1]) for h in range(2)] for c in range(4)]
        ue_ap = lambda c, a, b: U[:, OU + c * UE + a: OU + c * UE + b]
        tu = [[fw.arena_tk(OU + c * UE + 32 + HALF[h][0], OU + c * UE + 32 + HALF[h][1]) for h in range(2)] for c in range(4)]
        tuh = [fw.arena_tk(OU + c * UE, OU + c * UE + 32) for c in range(4)]
        ce_ap = lambda c, a, b: U[:, OCU + c * CE + a: OCU + c * CE + b]
        tcu = [[fw.arena_tk(OCU + c * CE + 4 + HALF[h][0], OCU + c * CE + 4 + HALF[h][1]) for h in range(2)] for c in range(4)]
        tcuh = [fw.arena_tk(OCU + c * CE, OCU + c * CE + 4) for c in range(4)]
        fw.finish_arena()

        kTe = [sb(f"kTe{l}", [128, (NT + 1) * 128], BF16) for l in range(2)]
        tkk = [[Tk() for _ in range(NT + 1)] for _ in range(2)]
        vE = [sb(f"vE{l}", [128, NT + 1, 128], BF16) for l in range(2)]
        tvv = [[Tk() for _ in range(NT + 1)] for _ in range(2)]
        kf = sb("kf", [128, GC], F32)
        tkf = [Tk(), Tk()]
        utail = [sb(f"utail{l}", [128, 4, 32], BF16) for l in range(2)]
        tut = [Tk(), Tk()]
        cutail = [sb(f"cutail{l}", [128, 4, 4], BF16) for l in range(2)]
        tct = [Tk(), Tk()]
        ropeT = sb("ropeT", [128, 2, GC], F32)
        trope = Tk()
        cst = sb("cst", [128, C_SZ], F32)
        tcst = Tk()
        msk = sb("msk", [128, 2048], BF16)
        tmsk = Tk()
        ones = sb("ones", [128, 128], BF16)
        tones = Tk()
        esb = sb("esb", [128, 512], F32)
        tesb = Tk()
        esk = sb("esk", [128, 8], F32)
        wsl = [sb(f"wsl{i}", [128, UNIT * 128], BF16) for i in range(NSLOT)]
        twsl = [Tk() for _ in range(NSLOT)]
        stage = [sb(f"stage{i}", [128, D], F32) for i in range(2)]
        tstage = [Tk(), Tk()]
        xstage = [sb(f"xstage{i}", [128, D], F32) for i in range(2)]
        txstage = [Tk(), Tk()]
        spec = sb("spec", [128, 1280], F32)
        tspec = [Tk() for _ in range(4)]
        u32 = sb("u32", [128, 4, 128], F32)
        tu32 = Tk()
        cu32 = sb("cu32", [128, 4, 128], F32)
        tcu32 = Tk()
        tmpb = [sb(f"tmp{i}", [128, 512], F32) for i in range(NTMP)]
        ttmp = [Tk() for _ in range(NTMP)]
        NSQ = 4
        sqb = [sb(f"sq{i}", [128, 384], BF16) for i in range(NSQ)]
        tsq = [Tk() for _ in range(NSQ)]
        identb = sb("identb", [128, 128], BF16)
        tidb = Tk()
        pTb = [sb(f"pT{i}", [128, 512], BF16) for i in range(4)]
        tpT = [Tk() for _ in range(4)]
        dw = sb("dw", [128, 4, GC], F32)
        tdw = [[Tk(), Tk()] for _ in range(4)]
        dwb = sb("dwb", [128, 4, GC], BF16)
        tdwb = [[Tk(), Tk()] for _ in range(4)]
        dw2 = sb("dw2", [128, 4, GC], BF16)
        tdw2 = [[Tk(), Tk()] for _ in range(4)]
        uS = sb("uS", [128, 4, 16, 38], BF16)
        tuS = [Tk() for _ in range(4)]
        cuS = sb("cuS", [128, 4, 16, 10], BF16)
        tcuS = [Tk() for _ in range(4)]
        ckT = sb("ckT", [128, 16, 128], BF16)
        tckT = Tk()
        cvb = sb("cvb", [128, 16, 128], BF16)
        tcvb = Tk()
        ckst = sb("ckst", [128, 4, 128], F32)
        tckst = Tk()

        banks = [st.enter_context(nc.psum_tensor(f"pb{i}", [128, 512], F32)) for i in range(8)]
        tbank = [Tk(excl=True) for _ in range(8)]
        ring = {"pb": 0, "tmp": 0, "sq": 0, "pT": 0, "stage": 0, "dg": 0}
        NRING = 6

        def pb():
            i = ring["pb"]
            ring["pb"] = (i + 1) % NRING
            return banks[i], tbank[i]

        def tmp():
            i = ring["tmp"]
            ring["tmp"] = (i + 1) % NTMP
            return tmpb[i], ttmp[i]

        def rr(name, n):
            i = ring[name]
            ring[name] = (i + 1) % n
            return i

        def MM(out, lhsT, rhs, start, stop, reads, writes, signal, sgc=False):
            fw.op("pe", lambda e: e.matmul(out, lhsT=lhsT, rhs=rhs, start=start, stop=stop, skip_group_check=sgc),
                  reads=reads, writes=writes, signal=signal)

        def TR(out, in_, idn, reads, writes, signal):
            fw.op("pe", lambda e: e.transpose(out=out, in_=in_, identity=idn), reads=reads, writes=writes, signal=signal)

        def ACT(out, in_, func, reads, writes, scale=1.0, bias=0.0):
            fw.op("act", lambda e: e.activation(out=out, in_=in_, func=func, bias=bias, scale=scale),
                  reads=reads, writes=writes)

        def TT(out, in0, in1, op, reads, writes, eng="dve"):
            fw.op(eng, lambda e: e.tensor_tensor(out=out, in0=in0, in1=in1, op=op), reads=reads, writes=writes)

        def STT(out, in0, scalar, in1, op0, op1, reads, writes):
            fw.op("dve", lambda e: e.scalar_tensor_tensor(out=out, in0=in0, scalar=scalar, in1=in1, op0=op0, op1=op1),
                  reads=reads, writes=writes)

        def TS(out, in0, s1, s2, op0, op1, reads, writes, eng="dve"):
            fw.op(eng, lambda e: e.tensor_scalar(out=out, in0=in0, scalar1=s1, scalar2=s2, op0=op0, op1=op1),
                  reads=reads, writes=writes)

        def CP(out, in_, reads, writes, eng="dve"):
            fw.op(eng, lambda e: e.tensor_copy(out=out, in_=in_), reads=reads, writes=writes)

        def RCP(out, in_, reads, writes):
            fw.op("dve", lambda e: e.reciprocal(out=out, in_=in_), reads=reads, writes=writes)

        def DMA(eng, out, in_, reads, writes, key):
            fw.op(eng, lambda e: e.dma_start(out=out, in_=in_), reads=reads, writes=writes, dma=key)

        ident = cst[:, C_ID:C_ID + 128]

        def prm(l, off, n=1):
            o = C_PRM + l * P_SZ + off
            return cst[:, o:o + n]

        wstate = {"cnt": 0, "g": 0}

        def wnext(spec_):
            c = wstate["cnt"]
            if wstate["g"] == 0:
                wspecs.append(spec_)
            u = c // UNIT
            s = u % NSLOT
            if c % UNIT == 0:
                DMA("pool", wsl[s][:], wdr[u], [], [twsl[s]], f"w{s}")
            wstate["cnt"] = c + 1
            o = (c % UNIT) * 128
            return wsl[s][:, o:o + 128], twsl[s]

        def project(l, mat, nk, rows_fn, cols, rhs_fn, rtk_fn):
            wts = [wnext((mat, l, rows_fn(kc), cols)) for kc in range(nk)]
            outs = [pb(), pb()]
            for kc in range(nk):
                for h, (a, b) in enumerate(HALF):
                    bk, tb = outs[h]
                    MM(bk[:, 0:b - a], wts[kc][0], rhs_fn(kc, a, b), kc == 0, kc == nk - 1,
                       [wts[kc][1], rtk_fn(kc, h)], [tb], kc == nk - 1)
            return outs

        nat = lambda kc: np.arange(kc * 128, (kc + 1) * 128)
        hT_rhs = lambda kc, a, b: hT[:, kc, a:b]
        hT_tk = lambda kc, h: th[kc][h]

        pcache = {}

        def proj_h(l, mat, cols):
            key = (l, mat, int(cols[0]), int(cols[-1]))
            if key in pcache:
                return pcache.pop(key)
            return project(l, mat, 8, nat, cols, hT_rhs, hT_tk)

        def preproject(l, mat, cols_list):
            n = len(cols_list)
            wts = {}
            outs = [[pb(), pb()] for _ in range(n)]
            for kc in range(8):
                for j, cols in enumerate(cols_list):
                    wts[(j, kc)] = wnext((mat, l, nat(kc), cols))
                for j in range(n):
                    for h, (a, b) in enumerate(HALF):
                        bk, tb = outs[j][h]
                        MM(bk[:, 0:b - a], wts[(j, kc)][0], hT[:, kc, a:b], kc == 0, kc == 7,
                           [wts[(j, kc)][1], th[kc][h]], [tb], kc == 7)
            assert not pcache
            for j, cols in enumerate(cols_list):
                pcache[(l, mat, int(cols[0]), int(cols[-1]))] = outs[j]

        def tiles_in(h):
            return (0, 1, 2) if h == 0 else (3, 4)

        def load_consts():
            DMA("sp", cst[:], cst_d, [], [tcst], "c0")
            DMA("pool", msk[:], msk_d, [], [tmsk], "c1")
            fw.op("dve", lambda e: e.memset(ones[:], 1.0), writes=[tones])
            CP(identb[:], cst[:, C_ID:C_ID + 128], [tcst], [tidb])

        xpre = {}

        def load_x_dma(g, i):
            s_ = i % 2
            gt = g * NT + i
            DMA("sp", xstage[s_][:], xin[gt * 128:(gt + 1) * 128, :], [], [txstage[s_]], f"xs{s_}")
            xpre[(g, i)] = True

        def prefetch_x(g):
            DMA("sp", ropeT[:], rope_d[:, :, g * GC:(g + 1) * GC].rearrange("t p c -> p t c"), [], [trope], "rope")
            xpre[("rope", g)] = True
            load_x_dma(g, 0)
            load_x_dma(g, 1)

        def load_x(g):
            if ("rope", g) not in xpre:
                DMA("sp", ropeT[:], rope_d[:, :, g * GC:(g + 1) * GC].rearrange("t p c -> p t c"), [], [trope], "rope")
            for i in range(min(2, NT)):
                if (g, i) not in xpre:
                    load_x_dma(g, i)
            for i in range(NT):
                s_ = i % 2
                for bk4 in range(2):
                    bk, tb = pb()
                    for j in range(4):
                        kc = bk4 * 4 + j
                        TR(bk[:, j * 128:(j + 1) * 128], xstage[s_][:, kc * 128:(kc + 1) * 128], ident,
                           [txstage[s_], tcst], [tb], j == 3)
                    eng = "act" if bk4 == 0 else "dve"
                    outap = xT[:, bk4 * 4:(bk4 + 1) * 4, i * 128:(i + 1) * 128]
                    inap = bk[:, 0:512].rearrange("p (a b) -> p a b", b=128)
                    wr = [tx[kc][half_of(i)] for kc in range(bk4 * 4, bk4 * 4 + 4)]
                    if eng == "act":
                        ACT(outap, inap, AF.Copy, [tb], wr)
                    else:
                        CP(outap, inap, [tb], wr)
                if i + 2 < NT:
                    load_x_dma(g, i + 2)

        nst = {"n": 0}

        def norm_chunk(kc):
            first = nst["n"] == 0
            last = nst["n"] == 7
            nst["n"] = (nst["n"] + 1) % 8
            pend = []
            for h, (a, b) in enumerate(HALF):
                w = b - a
                s_ = rr("sq", NSQ)
                ACT(sqb[s_][:, 0:w], xT[:, kc, a:b], AF.Square, [tx[kc][h]], [tsq[s_]])
                pend.append((h, w, s_, first, last))
            return pend

        def norm_mm(pend):
            for h, w, s_, first, last in pend:
                MM(banks[6 + h][:, 0:w], ones[:], sqb[s_][:, 0:w], first, last, [tones, tsq[s_]], [tbank[6 + h]], True)

        def norm_finish(goff_ap, dst_fn, dst_tk):
            assert nst["n"] == 0
            rds = []
            for h, (a, b) in enumerate(HALF):
                w = b - a
                bk, tb = banks[6 + h], tbank[6 + h]
                rs, trs = tmp()
                ACT(rs[:, 0:w], bk[:, 0:w], AF.Sqrt, [tb], [trs], scale=1.0 / D, bias=EPS)
                rds.append((rs, trs))
            for h, (a, b) in enumerate(HALF):
                w = b - a
                rs, trs = rds[h]
                rd, trd = tmp()
                RCP(rd[:, 0:w], rs[:, 0:w], [trs], [trd])
                rds[h] = (rd, trd)
            for kc in range(8):
                for h, (a, b) in enumerate(HALF):
                    w = b - a
                    rd, trd = rds[h]
                    STT(dst_fn(kc, a, b), xT[:, kc, a:b], goff_ap(kc), rd[:, 0:w], ALU.mult, ALU.mult,
                        [tx[kc][h], trd, tcst], [dst_tk(kc, h)])

        def norm(goff_ap, dst_fn, dst_tk):
            for kc in range(8):
                norm_mm(norm_chunk(kc))
            norm_finish(goff_ap, dst_fn, dst_tk)

        class Lag:
            def __init__(self):
                self.p = None

            def push(self, pend):
                if self.p is not None:
                    norm_mm(self.p)
                self.p = pend

            def flush(self):
                if self.p is not None:
                    norm_mm(self.p)
                self.p = None

        def swapcols(base):
            d = np.arange(64)
            sw = np.where(d < 32, d + 32, d - 32)
            return lambda hA, hB: np.concatenate([base + hA * 64 + sw, base + hB * 64 + sw])

        def rope_evac(pq, pqs, dst_fn, wr_fn, dt_reads=()):
            for h, (a, b) in enumerate(HALF):
                w = b - a
                t1, tt1 = tmp()
                TT(t1[:, 0:w], pq[h][0][:, 0:w], ropeT[:, 0, a:b], ALU.mult, [pq[h][1], trope], [tt1])
                t2, tt2 = tmp()
                TT(t2[:, 0:w], pqs[h][0][:, 0:w], ropeT[:, 1, a:b], ALU.mult, [pqs[h][1], trope], [tt2])
                TT(dst_fn(a, b), t1[:, 0:w], t2[:, 0:w], ALU.add, [tt1, tt2], wr_fn(h))

        def special_of(g):
            if g == 0:
                return (0, "S")
            if g == NG - 1:
                return (NT - 1, "L")
            return None

        WSTOP = int(os.environ.get('WSTOP', '99'))

        def win_phase(g, l):
            spc = special_of(g)
            if g > 0:
                for c in range(4):
                    CP(ue_ap(c, 0, 32), utail[l][:, c, :], [tut[l]], [tuh[c]], eng="act" if False else "dve")
                    CP(ce_ap(c, 0, 4), cutail[l][:, c, :], [tct[l]], [tcuh[c]])
            preproject(l, "w_in", [CAB + np.arange(128), CGB + np.arange(128), CAB + 128 + np.arange(128)])
            for c in range(4):
                pa = proj_h(l, "w_in", CAB + c * 128 + np.arange(128))
                pg = proj_h(l, "w_in", CGB + c * 128 + np.arange(128))
                for h, (a, b) in enumerate(HALF):
                    w = b - a
                    sg, tsg = tmp()
                    ACT(sg[:, 0:w], pg[h][0][:, 0:w], AF.Sigmoid, [pg[h][1]], [tsg])
                    TT(ue_ap(c, 32 + a, 32 + b), pa[h][0][:, 0:w], sg[:, 0:w], ALU.mult, [pa[h][1], tsg], [tu[c][h]])
                    if spc is not None and half_of(spc[0]) == h:
                        i = spc[0]
                        o = i * 128 - a
                        TT(u32[:, c, :], pa[h][0][:, o:o + 128], sg[:, o:o + 128], ALU.mult, [pa[h][1], tsg], [tu32])
            for c in range(4):
                p1 = proj_h(l, "w_in", SCB + c * 128 + np.arange(128))
                p2 = proj_h(l, "w_in", SXB + c * 128 + np.arange(128))
                for h, (a, b) in enumerate(HALF):
                    w = b - a
                    t1, tt1 = tmp()
                    ACT(t1[:, 0:w], p1[h][0][:, 0:w], AF.Copy, [p1[h][1]], [tt1])
                    TT(ce_ap(c, 4 + a, 4 + b), p2[h][0][:, 0:w], t1[:, 0:w], ALU.mult, [p2[h][1], tt1], [tcu[c][h]])
                    if spc is not None and half_of(spc[0]) == h:
                        i = spc[0]
                        o = i * 128 - a
                        TT(cu32[:, c, :], p2[h][0][:, o:o + 128], t1[:, o:o + 128], ALU.mult, [p2[h][1], tt1], [tcu32])
            for c in range(4):
                pS = proj_h(l, "w_in", SBB + c * 128 + np.arange(128))
                for h, (a, b) in enumerate(HALF):
                    ACT(sb_ap(c, a, b), pS[h][0][:, 0:b - a], AF.Copy, [pS[h][1]], [tsb[c][h]])
            if spc is not None:
                for src, tsrc, col0, tdst in ((u32, tu32, 256, tspec[2]), (cu32, tcu32, 768, tspec[3])):
                    bk, tb = pb()
                    for c in range(4):
                        TR(bk[:, c * 128:(c + 1) * 128], src[:, c, :], ident, [tsrc, tcst], [tb], c == 3)
                    CP(spec[:, col0:col0 + 512], bk[:, 0:512], [tb], [tdst])
            if g < NG - 1:
                for c in range(4):
                    CP(utail[l][:, c, :], ue_ap(c, GC, GC + 32), [tu[c][1]], [tut[l]])
                    CP(cutail[l][:, c, :], ce_ap(c, GC, GC + 4), [tcu[c][1]], [tct[l]])

        def win_q(g, l):
            for c in range(4):
                cols = np.concatenate([QB + c * 64 + np.arange(64), QB + (4 + c) * 64 + np.arange(64)])
                pq = proj_h(l, "w_in", cols)
                pqs = proj_h(l, "w_in", swapcols(QB)(c, 4 + c))
                rope_evac(pq, pqs, lambda a, b, c=c: q_ap(c, a, b), lambda h, c=c: [tq[c][i] for i in tiles_in(h)])

        def win_phase2(g, l, part):
            spc = special_of(g)
            if part == 0:
                win_q(g, l)
                return
            pk = proj_h(l, "w_in", KB + np.arange(128))
            pks = proj_h(l, "w_in", swapcols(KB)(0, 1))
            rope_evac(pk, pks, lambda a, b: kf[:, a:b], lambda h: [tkf[h]])
            for h, (a, b) in enumerate(HALF):
                ACT(kTe[l][:, 128 + a:128 + b], kf[:, a:b], AF.Copy, [tkf[h]], [tkk[l][1 + i] for i in tiles_in(h)])
            if spc is not None:
                i, kind = spc
                bk, tb = pb()
                TR(bk[:, 0:128], kf[:, i * 128:(i + 1) * 128], ident, [tkf[half_of(i)], tcst], [tb], True)
                ACT(spec[:, 0:128], bk[:, 0:128], AF.Copy, [tb], [tspec[0]])
            wv = [wnext(("w_in", l, nat(kc), VB + np.arange(128))) for kc in range(8)]
            vb2 = [pb(), pb()]
            for i in range(NT):
                bk, tb = vb2[i // 4]
                for kc in range(8):
                    MM(bk[:, (i % 4) * 128:(i % 4 + 1) * 128], hT[:, kc, i * 128:(i + 1) * 128], wv[kc][0], kc == 0, kc == 7,
                       [th[kc][half_of(i)], wv[kc][1]], [tb], kc == 7)
            ACT(vE[l][:, 1:5, :], vb2[0][0][:, 0:512].rearrange("p (a b) -> p a b", b=128), AF.Copy, [vb2[0][1]],
                [tvv[l][1 + i] for i in range(4)])
            ACT(vE[l][:, 5, :], vb2[1][0][:, 0:128], AF.Copy, [vb2[1][1]], [tvv[l][5]])
            if spc is not None:
                i, kind = spc
                bk, tb = vb2[i // 4]
                ACT(spec[:, 128:256], bk[:, (i % 4) * 128:(i % 4 + 1) * 128], AF.Copy, [tb], [tspec[1]])
            if spc is not None:
                special_out(g, l, spc[1])

        def special_out(g, l, kind):
            rall = list(tspec)
            if kind == "L":
                DMA("sp", kp_d[l], spec[:, 0:128], [tspec[0]], [], "o0")
                DMA("sp", vp_d[l], spec[:, 128:256], [tspec[1]], [], "o1")
                DMA("sp", cp_d[l], spec[98:128, 256:768], [tspec[2]], [], "o2")
                DMA("sp", sp_d[l], spec[126:128, 768:1280], [tspec[3]], [], "o3")
            else:
                for b in range(16):
                    DMA("sp", ks_d[l, b, 120:128, :], spec[b * 8:(b + 1) * 8, 0:128], [tspec[0]], [], "o0")
                    DMA("sp", vs_d[l, b, 120:128, :], spec[b * 8:(b + 1) * 8, 128:256], [tspec[1]], [], "o1")
                    DMA("sp", cs_d[l, b, 22:30, :], spec[b * 8:(b + 1) * 8, 256:768], [tspec[2]], [], "o2")
                    DMA("sp", ss_d[l, b, 0:2, :], spec[b * 8 + 6:b * 8 + 8, 768:1280], [tspec[3]], [], "o3")
                DMA("sp", ks_d[l, :, 0:120, :], ck_d[l, :, 8:128, :], [], [], "o4")
                DMA("sp", vs_d[l, :, 0:120, :], cv_d[l, :, 8:128, :], [], [], "o5")
                DMA("sp", cs_d[l, :, 0:22, :], stc_d[l].rearrange("(b t) c -> b t c", t=30)[:, 8:30, :], [], [], "o6")

        def setup_layer(g, l):
            ACT(esk[:, 0:4], prm(l, P_SINK, 4), AF.Exp, [tcst, tesb], [tesb])
            fw.op("dve", lambda e: e.memset(esb[:], 0.0), reads=[], writes=[tesb])
            for c in range(4):
                TS(esb[:, c * 128:(c + 1) * 128], esb[:, c * 128:(c + 1) * 128], esk[:, c:c + 1], None, ALU.add, ALU.bypass,
                   [tesb], [tesb])

        def load_sample_state(l):
            for b4 in range(4):
                DMA("sp", ckst[:], ck_d[l, b4 * 4:(b4 + 1) * 4].rearrange("b j f -> j b f"), [], [tckst], "ckst")
                bk, tb = pb()
                for j in range(4):
                    TR(bk[:, j * 128:(j + 1) * 128], ckst[:, j, :], ident, [tckst, tcst], [tb], j == 3)
                ACT(ckT[:, b4 * 4:(b4 + 1) * 4, :], bk[:, 0:512].rearrange("p (a b) -> p a b", b=128), AF.Copy, [tb], [tckT])
            DMA("pool", cvb[:], cv_d[l].rearrange("b j f -> j b f"), [], [tcvb], "cvb")
            for b4 in range(4):
                s = rr("stage", 2)
                DMA("sp", stage[s][0:120, 0:512], stc_d[l, b4 * 120:(b4 + 1) * 120, :], [], [tstage[s]], f"st{s}")
                bk, tb = pb()
                for c in range(4):
                    TR(bk[:, c * 120:(c + 1) * 120], stage[s][0:120, c * 128:(c + 1) * 128], cst[0:120, C_ID:C_ID + 120],
                       [tstage[s], tcst], [tb], c == 3)
                for c in range(4):
                    ACT(uS[:, c, b4 * 4:(b4 + 1) * 4, 0:30], bk[:, c * 120:(c + 1) * 120].rearrange("p (a b) -> p a b", b=30),
                        AF.Copy, [tb], [tuS[c]])
            s = rr("stage", 2)
            DMA("sp", stage[s][0:32, 0:512], sts_d[l], [], [tstage[s]], f"st{s}")
            bk, tb = pb()
            for c in range(4):
                TR(bk[:, c * 32:(c + 1) * 32], stage[s][0:32, c * 128:(c + 1) * 128], cst[0:32, C_ID:C_ID + 32],
                   [tstage[s], tcst], [tb], c == 3)
            for c in range(4):
                ACT(cuS[:, c, :, 0:2], bk[:, c * 32:(c + 1) * 32].rearrange("p (a b) -> p a b", b=2), AF.Copy, [tb], [tcuS[c]])

        def attn_finish(l, i, ob, tob, sbk, tsbk):
            den, tden = tmp()
            TT(den[:, 0:512], sbk[:, 0:512], esb[:, 0:512], ALU.add, [tsbk, tesb], [tden])
            rden, trden = tmp()
            RCP(rden[:, 0:512], den[:, 0:512], [tden], [trden])
            outap = U[:, OQ + i * 128: OQ + i * 128 + 4 * GC].rearrange("p (c x) -> p c x", x=GC)[:, :, 0:128]
            TT(outap, ob[:, 0:512].rearrange("p (c x) -> p c x", x=128), rden[:, 0:512].rearrange("p (c x) -> p c x", x=128),
               ALU.mult, [tob, trden], [tq[c][i] for c in range(4)])

        def score_evac(bk, tb, mcol):
            sm, tsm = tmp()
            STT(sm[:, 0:512], bk[:, 0:512], 480.0, msk[:, mcol:mcol + 512], ALU.min, ALU.add, [tb, tmsk], [tsm])
            p = rr("pT", 4)
            ACT(pTb[p][:, 0:512], sm[:, 0:512], AF.Exp, [tsm], [tpT[p]], scale=0.125)
            return pTb[p], tpT[p]

        def attn_scores(g, l, i, cp):
            kprev = lambda r0: kTe[l][r0:r0 + 64, i * 128:(i + 1) * 128]
            kcur = lambda r0: kTe[l][r0:r0 + 64, (i + 1) * 128:(i + 2) * 128]
            bks = [pb(), pb()]
            for cc in range(2):
                c = cp * 2 + cc
                for hh in range(2):
                    r0 = hh * 64
                    bk_, tb_ = bks[hh]
                    qa = U[r0:r0 + 64, OQ + c * GC + i * 128: OQ + c * GC + (i + 1) * 128]
                    MM(bk_[:, cc * 256:cc * 256 + 128], kprev(r0), qa, True, True, [tkk[l][i], tq[c][i]], [tb_], False)
                    MM(bk_[:, cc * 256 + 128:cc * 256 + 256], kcur(r0), qa, True, True, [tkk[l][i + 1], tq[c][i]], [tb_], cc == 1)
            return bks

        def attn_evac(g, l, i, cp, bks):
            mcol = M_H if (g == 0 and i == 3) else M_P
            return [score_evac(bks[hh][0], bks[hh][1], mcol) for hh in range(2)]

        def attn_pv(g, l, i, cp, pts):
            ob, tob = pb()
            for cc in range(2):
                for hh in range(2):
                    r0 = hh * 64
                    pT, tp_ = pts[hh]
                    MM(ob[r0:r0 + 64, cc * 128:(cc + 1) * 128], vE[l][:, i, r0:r0 + 64], pT[:, cc * 256:cc * 256 + 128], True, False,
                       [tvv[l][i], tp_], [tob], False)
                    MM(ob[r0:r0 + 64, cc * 128:(cc + 1) * 128], vE[l][:, i + 1, r0:r0 + 64], pT[:, cc * 256 + 128:cc * 256 + 256], False, True,
                       [tvv[l][i + 1], tp_], [tob], False)
                for hh in range(2):
                    r0 = hh * 64
                    pT, tp_ = pts[hh]
                    MM(ob[r0:r0 + 64, 256 + cc * 128:256 + (cc + 1) * 128], ones[:, 0:64], pT[:, cc * 256:cc * 256 + 128], True, False,
                       [tones, tp_], [tob], False)
                    MM(ob[r0:r0 + 64, 256 + cc * 128:256 + (cc + 1) * 128], ones[:, 0:64], pT[:, cc * 256 + 128:cc * 256 + 256], False, True,
                       [tones, tp_], [tob], cc == 1 and hh == 1)
            return ob, tob

        def attn_fin(l, i, cp, ob, tob):
            den, tden = tmp()
            TT(den[:, 0:256], ob[:, 256:512], esb[:, cp * 256:(cp + 1) * 256], ALU.add, [tob, tesb], [tden])
            rden, trden = tmp()
            RCP(rden[:, 0:256], den[:, 0:256], [tden], [trden])
            o0 = OQ + cp * 2 * GC + i * 128
            outap = U[:, o0: o0 + 2 * GC].rearrange("p (c x) -> p c x", x=GC)[:, :, 0:128]
            TT(outap, ob[:, 0:256].rearrange("p (c x) -> p c x", x=128), rden[:, 0:256].rearrange("p (c x) -> p c x", x=128),
               ALU.mult, [tob, trden], [tq[cp * 2][i], tq[cp * 2 + 1][i]])

        def attn_prompt_tiles(g, l, tiles, filler=()):
            steps = [(i, cp) for i in tiles for cp in range(2)]
            n = len(steps)
            filler = list(filler)
            tail_items = filler[-1:]
            filler = filler[:-1]
            per = -(-len(filler) // max(1, n - 1))
            sc = {0: attn_scores(g, l, *steps[0])}
            ev = {0: attn_evac(g, l, *steps[0], sc[0])}
            for k in range(n):
                i, cp = steps[k]
                if k + 1 < n:
                    sc[k + 1] = attn_scores(g, l, *steps[k + 1])
                    ev[k + 1] = attn_evac(g, l, *steps[k + 1], sc[k + 1])
                ob, tob = attn_pv(g, l, i, cp, ev[k])
                attn_fin(l, i, cp, ob, tob)
                for f_ in filler[k * per:(k + 1) * per]:
                    f_()
            for f_ in filler[n * per:]:
                f_()
            for f_ in tail_items:
                f_()

        def attn_sample_tile(l):
            ob, tob = pb()
            sbk, tsbk = pb()
            bn = [pb(), pb()]
            bc = [pb(), pb()]
            for c in range(4):
                for hh in range(2):
                    r0 = hh * 64
                    qa = U[r0:r0 + 64, OQ + c * GC: OQ + c * GC + 128]
                    MM(bn[hh][0][:, c * 128:(c + 1) * 128], kTe[l][r0:r0 + 64, 128:256], qa, True, True,
                       [tkk[l][1], tq[c][0]], [bn[hh][1]], c == 3)
                    for b in range(16):
                        MM(bc[hh][0][:, c * 128 + b * 8:c * 128 + b * 8 + 8], ckT[r0:r0 + 64, b, :], qa[:, b * 8:(b + 1) * 8], True, True,
                           [tckT, tq[c][0]], [bc[hh][1]], c == 3 and b == 15)
            pn = [score_evac(bn[hh][0], bn[hh][1], M_SN) for hh in range(2)]
            pc = [score_evac(bc[hh][0], bc[hh][1], M_SC) for hh in range(2)]
            for c in range(4):
                for hh in range(2):
                    r0 = hh * 64
                    MM(ob[r0:r0 + 64, c * 128:(c + 1) * 128], vE[l][:, 1, r0:r0 + 64], pn[hh][0][:, c * 128:(c + 1) * 128], True, False,
                       [tvv[l][1], pn[hh][1]], [tob], False)
                    for b in range(16):
                        MM(ob[r0:r0 + 64, c * 128 + b * 8:c * 128 + b * 8 + 8], cvb[:, b, r0:r0 + 64],
                           pc[hh][0][:, c * 128 + b * 8:c * 128 + b * 8 + 8], False, b == 15, [tcvb, pc[hh][1]], [tob], False)
                for hh in range(2):
                    r0 = hh * 64
                    MM(sbk[r0:r0 + 64, c * 128:(c + 1) * 128], ones[:, 0:64], pn[hh][0][:, c * 128:(c + 1) * 128], True, False,
                       [tones, pn[hh][1]], [tsbk], False)
                    MM(sbk[r0:r0 + 64, c * 128:(c + 1) * 128], ones[:, 0:64], pc[hh][0][:, c * 128:(c + 1) * 128], False, True,
                       [tones, pc[hh][1]], [tsbk], c == 3 and hh == 1)
            attn_finish(l, 0, ob, tob, sbk, tsbk)

        def conv_items(g, l, c, ntap, w_off, ext_ap, hoff, t_hist, t_main, sext, t_sext, evac_fn):
            outs = [(banks[6], tbank[6]), (banks[7], tbank[7])]

            def tap(tau):
                dg_, tdg_ = wnext(("dww" if ntap == 31 else "scw", l, tau, c))
                sh = tau - (ntap - 1) + hoff
                for h, (a, b) in enumerate(HALF):
                    bk, tb = outs[h]
                    a0 = a
                    sgc = False
                    if g == 0 and h == 0:
                        MM(bk[:, 0:128], dg_, sext[:, c, :, tau:tau + 8], tau == 0, tau == ntap - 1,
                           [tdg_, t_sext[c]], [tb], False, sgc=True)
                        a0 = 128
                        sgc = True
                    rds = [tdg_, t_main[c][h]] + ([t_hist[c]] if h == 0 else [t_main[c][0]])
                    MM(bk[:, a0 - a:b - a], dg_, ext_ap(c, a0 + sh, b + sh), (tau == 0) and not sgc, tau == ntap - 1,
                       rds, [tb], (tau == ntap - 1) or h == 1, sgc=sgc)

            items = [(lambda tau=tau: tap(tau)) for tau in range(ntap)]
            items.append(lambda: evac_fn(c, outs))
            return items

        def conf_evac(l, c, outs):
            bias = prm(l, P_DWB + c)
            (a, b) = HALF[0]
            bk, tb = outs[0]
            w = b - a
            fw.op("act", lambda e, bk=bk, w=w, a=a, b=b, c=c, bias=bias: e.activation(
                out=dw[:, c, a:b], in_=bk[:, 0:w], func=AF.Identity, bias=bias, scale=1.0), reads=[tb, tcst], writes=[tdw[c][0]])
            fw.op("act", lambda e, bk=bk, w=w, a=a, b=b, c=c, bias=bias: e.activation(
                out=dwb[:, c, a:b], in_=bk[:, 0:w], func=AF.Identity, bias=bias, scale=1.0), reads=[tb, tcst], writes=[tdwb[c][0]])
            fw.op("act", lambda e, bk=bk, w=w, a=a, b=b, c=c, bias=bias: e.activation(
                out=dw2[:, c, a:b], in_=bk[:, 0:w], func=AF.Square, bias=bias, scale=1.0), reads=[tb, tcst], writes=[tdw2[c][0]])
            (a, b) = HALF[1]
            bk, tb = outs[1]
            w = b - a
            TS(dw[:, c, a:b], bk[:, 0:w], bias, None, ALU.add, ALU.bypass, [tb, tcst], [tdw[c][1]])
            TS(dwb[:, c, a:b], bk[:, 0:w], bias, None, ALU.add, ALU.bypass, [tb, tcst], [tdwb[c][1]])
            TT(dw2[:, c, a:b], dw[:, c, a:b], dw[:, c, a:b], ALU.mult, [tdw[c][1]], [tdw2[c][1]])

        def sconv_evac(l, c, outs):
            for h, (a, b) in enumerate(HALF):
                TT(sb_ap(c, a, b), outs[h][0][:, 0:b - a], sb_ap(c, a, b), ALU.mult, [outs[h][1], tsb[c][h]], [tsb[c][h]])

        def conv_work(g, l):
            if g == 0:
                for c in range(4):
                    CP(uS[:, c, :, 30:38], ue_ap(c, 32, 160).rearrange("p (s t) -> p s t", t=8), [tu[c][0]], [tuS[c]])
                    CP(cuS[:, c, :, 2:10], ce_ap(c, 4, 132).rearrange("p (s t) -> p s t", t=8), [tcu[c][0]], [tcuS[c]])
            items = []
            for c in range(4):
                items += conv_items(g, l, c, 31, P_DWW, ue_ap, 32, tuh, tu, uS, tuS, lambda c_, o_: conf_evac(l, c_, o_))
            items.append(lambda: conf_ln_both(g, l))
            for c in range(4):
                items += conv_items(g, l, c, 3, P_SCW, ce_ap, 4, tcuh, tcu, cuS, tcuS, lambda c_, o_: sconv_evac(l, c_, o_))
            return items

        def conf_ln_both(g, l):
            st_ = []
            for h, (a, b) in enumerate(HALF):
                w = b - a
                if h == 0:
                    bm, tbm, b2, tb2 = banks[6], tbank[6], banks[7], tbank[7]
                else:
                    (bm, tbm), (b2, tb2) = pb(), pb()
                for c in range(4):
                    MM(bm[:, 0:w], ones[:], dwb[:, c, a:b], c == 0, c == 3, [tones, tdwb[c][h]], [tbm], c == 3)
                for c in range(4):
                    MM(b2[:, 0:w], ones[:], dw2[:, c, a:b], c == 0, c == 3, [tones, tdw2[c][h]], [tb2], c == 3)
                st_.append(dict(a=a, b=b, w=w, bm=bm, tbm=tbm, b2=b2, tb2=tb2))
            for d_ in st_:
                d_["mean"], d_["tmean"] = tmp()
                ACT(d_["mean"][:, 0:d_["w"]], d_["bm"][:, 0:d_["w"]], AF.Copy, [d_["tbm"]], [d_["tmean"]], scale=1.0 / 512)
            spare = []
            for d_ in st_:
                w = d_["w"]
                msq, tmsq = tmp()
                spare.append((msq, tmsq))
                TT(msq[:, 0:w], d_["mean"][:, 0:w], d_["mean"][:, 0:w], ALU.mult, [d_["tmean"]], [tmsq])
                d_["var"], d_["tvar"] = tmp()
                STT(d_["var"][:, 0:w], d_["b2"][:, 0:w], 1.0 / 512, msq[:, 0:w], ALU.mult, ALU.subtract, [d_["tb2"], tmsq], [d_["tvar"]])
            for d_ in st_:
                w = d_["w"]
                d_["rs"], d_["trs"] = tmp()
                spare.append((d_["rs"], d_["trs"]))
                ACT(d_["rs"][:, 0:w], d_["var"][:, 0:w], AF.Sqrt, [d_["tvar"]], [d_["trs"]], bias=EPS)
            for d_ in st_:
                w = d_["w"]
                RCP(d_["var"][:, 0:w], d_["rs"][:, 0:w], [d_["trs"]], [d_["tvar"]])
            t1s = {}
            for c in range(4):
                for h, d_ in enumerate(st_):
                    a, b, w = d_["a"], d_["b"], d_["w"]
                    t1, tt1 = spare[(c * 2 + h) % len(spare)]
                    TT(t1[:, 0:w], dw[:, c, a:b], d_["mean"][:, 0:w], ALU.subtract, [tdw[c][h], d_["tmean"]], [tt1])
                    TT(t1[:, 0:w], t1[:, 0:w], d_["var"][:, 0:w], ALU.mult, [tt1, d_["tvar"]], [tt1])
                    fw.op("act", lambda e, c=c, t1=t1, a=a, b=b, w=w: e.activation(
                        out=co_ap(c, a, b), in_=t1[:, 0:w], func=AF.Silu, bias=prm(l, P_LNB + c), scale=prm(l, P_LNG + c)),
                        reads=[tt1, tcst], writes=[tco[c][h]])

        def merge_phase(g, l):
            br_rhs = [
                (lambda kc, a, b: q_ap(kc, a, b), lambda kc, h: None),
                (lambda kc, a, b: co_ap(kc, a, b), lambda kc, h: tco[kc][h]),
                (lambda kc, a, b: sb_ap(kc, a, b), lambda kc, h: tsb[kc][h]),
            ]
            d64 = np.arange(64)
            wb_rows = [
                lambda kc: np.concatenate([kc * 64 + d64, (4 + kc) * 64 + d64]),
                nat, nat,
            ]
            for m in range(8):
                gts = {}
                for br in range(3):
                    pg = proj_h(l, "w_in", GLB + br * 1024 + m * 128 + np.arange(128))
                    gts[br] = []
                    for h, (a, b) in enumerate(HALF):
                        gt_, tgt = tmp()
                        ACT(gt_[:, 0:b - a], pg[h][0][:, 0:b - a], AF.Sigmoid, [pg[h][1]], [tgt])
                        gts[br].append((gt_, tgt))
                acc = [None, None]
                for n_, br in enumerate((0, 2, 1)):
                    wts = [wnext((f"wb{br}", l, wb_rows[br](kc), m * 128 + np.arange(128))) for kc in range(4)]
                    outs = [pb(), pb()]
                    for kc in range(4):
                        for h, (a, b) in enumerate(HALF):
                            if br == 0:
                                rtk = [tq[kc][i] for i in tiles_in(h)]
                            else:
                                rtk = [br_rhs[br][1](kc, h)]
                            MM(outs[h][0][:, 0:b - a], wts[kc][0], br_rhs[br][0](kc, a, b), kc == 0, kc == 3,
                               [wts[kc][1]] + rtk, [outs[h][1]], kc == 3)
                    for h, (a, b) in enumerate(HALF):
                        w = b - a
                        gt_, tgt = gts[br][h]
                        if n_ == 0:
                            t, tt_ = tmp()
                            TT(t[:, 0:w], outs[h][0][:, 0:w], gt_[:, 0:w], ALU.mult, [outs[h][1], tgt], [tt_])
                            acc[h] = (t, tt_)
                        else:
                            t2, tt2 = tmp()
                            TT(t2[:, 0:w], outs[h][0][:, 0:w], gt_[:, 0:w], ALU.mult, [outs[h][1], tgt], [tt2])
                            t, tt_ = acc[h]
                            if n_ == 1:
                                TT(t[:, 0:w], t[:, 0:w], t2[:, 0:w], ALU.add, [tt_, tt2], [tt_])
                            else:
                                TT(mg[:, m, a:b], t[:, 0:w], t2[:, 0:w], ALU.add, [tt_, tt2], [tmg[m][h]])
            lag = Lag()
            for m in range(8):
                po = project(l, "w_out", 8, nat, m * 128 + np.arange(128), lambda kc, a, b: mg[:, kc, a:b], lambda kc, h: tmg[kc][h])
                for h, (a, b) in enumerate(HALF):
                    TT(xT[:, m, a:b], po[h][0][:, 0:b - a], xT[:, m, a:b], ALU.add, [po[h][1], tx[m][h]], [tx[m][h]])
                lag.push(norm_chunk(m))
            lag.flush()

        def ffn_phase(g, l):
            preproject(l, "w_ffn_in", [np.arange(128), DFF + np.arange(128), 128 + np.arange(128)])
            for j in range(NFF):
                pgt = proj_h(l, "w_ffn_in", j * 128 + np.arange(128))
                pup = proj_h(l, "w_ffn_in", DFF + j * 128 + np.arange(128))
                for h, (a, b) in enumerate(HALF):
                    w = b - a
                    sg, tsg = tmp()
                    ACT(sg[:, 0:w], pgt[h][0][:, 0:w], AF.Silu, [pgt[h][1]], [tsg])
                    TT(act_ap(j, a, b), pup[h][0][:, 0:w], sg[:, 0:w], ALU.mult, [pup[h][1], tsg], [tact[j][h]])
            lag = Lag()
            for m in range(8):
                po = project(l, "w_ffn_out", NFF, nat, m * 128 + np.arange(128), lambda kc, a, b: act_ap(kc, a, b),
                             lambda kc, h: tact[kc][h])
                for h, (a, b) in enumerate(HALF):
                    TT(xT[:, m, a:b], po[h][0][:, 0:b - a], xT[:, m, a:b], ALU.add, [po[h][1], tx[m][h]], [tx[m][h]])
                lag.push(norm_chunk(m))
            lag.flush()

        def carry_kv(g, l):
            if g < NG - 1:
                CP(kTe[l][:, 0:128], kTe[l][:, NT * 128:(NT + 1) * 128], [tkk[l][NT]], [tkk[l][0]])
                CP(vE[l][:, 0, :], vE[l][:, NT, :], [tvv[l][NT]], [tvv[l][0]])

        def final_out(g):
            norm_finish(lambda kc: cst[:, C_GF + kc:C_GF + kc + 1], lambda kc, a, b: xT[:, kc, a:b], lambda kc, h: tx[kc][h])
            for i in range(NT):
                gt = g * NT + i
                s = rr("stage", 2)
                for bk4 in range(2):
                    bk, tb = pb()
                    for j in range(4):
                        kc = bk4 * 4 + j
                        TR(bk[:, j * 128:(j + 1) * 128], xT[:, kc, i * 128:(i + 1) * 128], ident,
                           [tx[kc][half_of(i)], tcst], [tb], j == 3)
                    if bk4 == 0:
                        ACT(stage[s][:, 0:512], bk[:, 0:512], AF.Copy, [tb], [tstage[s]])
                    else:
                        CP(stage[s][:, 512:1024], bk[:, 0:512], [tb], [tstage[s]])
                DMA("sp", yout[gt * 128:(gt + 1) * 128, :], stage[s][:], [tstage[s]], [], f"st{s}")

        load_consts()
        for g in range(ng):
            wstate["g"] = g
            wstate["cnt"] = 0
            load_x(g)
            for l in range(2):
                setup_layer(g, l)
                if g == 0:
                    load_sample_state(l)
                n1 = (lambda kc, l=l: prm(l, P_G1 + kc), lambda kc, a, b: hT[:, kc, a:b], lambda kc, h: th[kc][h])
                if l == 0:
                    norm(*n1)
                else:
                    norm_finish(*n1)
                win_phase(g, l)
                win_phase2(g, l, 0)
                win_phase2(g, l, 1)
                cw = conv_work(g, l)
                if g == 0:
                    attn_sample_tile(l)
                    attn_prompt_tiles(g, l, range(1, NT), cw)
                else:
                    attn_prompt_tiles(g, l, range(NT), cw)
                carry_kv(g, l)
                merge_phase(g, l)
                if l == 1 and g + 1 < ng:
                    prefetch_x(g + 1)
                norm_finish(lambda kc, l=l: prm(l, P_G2 + kc), lambda kc, a, b: hT[:, kc, a:b], lambda kc, h: th[kc][h])
                ffn_phase(g, l)
                if g == 0 and l == 0:
                    TS(xT[:, :, 256:384], xT[:, :, 256:384], cst[:, C_FLAG:C_FLAG + 1], None, ALU.mult, ALU.bypass,
                       [tx[kc][0] for kc in range(8)] + [tcst], [tx[kc][0] for kc in range(8)])
            assert stop < 99 or wstate["cnt"] == NUNIT * UNIT, wstate["cnt"]
            final_out(g)
        fw.emit()
    return nc, wspecs


_CACHE = {}


def _get_program():
    if "p" not in _CACHE:
        _CACHE["p"] = build_program()
    return _CACHE["p"]


def _pack_weights(wspecs, mats):
    n = len(wspecs)
    out = np.zeros((158, 128, UNIT * 128), np.float32)
    for t, (mat, l, rows, cols) in enumerate(wspecs):
        W = mats[mat][l]
        if mat in ("dww", "scw"):
            tau, c = rows, cols
            blk = np.zeros((128, 128), np.float32)
            blk[np.arange(128), np.arange(128)] = W[tau, c * 128:(c + 1) * 128]
            out[t // UNIT, :, (t % UNIT) * 128:(t % UNIT + 1) * 128] = blk
            continue
        c0 = int(cols[0])
        if np.array_equal(cols, np.arange(c0, c0 + 128)):
            blk = W[:, c0:c0 + 128]
        else:
            blk = W[:, cols]
        r0 = int(rows[0])
        if np.array_equal(rows, np.arange(r0, r0 + 128)):
            blk = blk[r0:r0 + 128]
        else:
            blk = blk[rows]
        out[t // UNIT, :, (t % UNIT) * 128:(t % UNIT + 1) * 128] = blk
    return out


def _masks(first_half):
    j = np.arange(128)[:, None]
    q = np.arange(128)[None, :]
    prev = np.where(j > q, 0.0, NEG)
    cur = np.where(j <= q, 0.0, NEG)
    mP = np.concatenate([prev, cur, prev, cur], axis=1)
    pv = np.full((128, 128), NEG) if first_half else prev
    mH = np.concatenate([pv, cur, pv, cur], axis=1)
    sn = np.where(((j // 8) == (q // 8)) & ((j % 8) <= (q % 8)), 0.0, NEG)
    mSn = np.concatenate([sn] * 4, axis=1)
    t = np.arange(512)[None, :] % 8
    mSc = np.where(j >= t + 1, 0.0, NEG)
    return np.concatenate([mP, mH, mSn, mSc], axis=1).astype(np.float32)


def _rope_tables(pos):
    half = 32
    inv = (np.float32(10000.0) ** (-(np.arange(half, dtype=np.float32) / np.float32(half)))).astype(np.float32)
    ang = (pos.astype(np.float32)[None, :] * inv[:, None]).astype(np.float32)
    cs = np.cos(ang.astype(np.float64)).astype(np.float32)
    sn = np.sin(ang.astype(np.float64)).astype(np.float32)
    d = np.arange(128) % 64
    cosT = cs[d % 32]
    sinT = sn[d % 32] * np.where(d >= 32, 1.0, -1.0).astype(np.float32)[:, None]
    return np.stack([cosT, sinT]).astype(np.float32)


def kernel(x_prompt, x_sample, cache_k, cache_v, state_conf, state_sconv,
           norm1, w_in, sinks, conf_dw_w, conf_dw_b, conf_ln_g, conf_ln_b, sconv_w,
           w_branch, w_out, norm2, w_ffn_in, w_ffn_out, final_norm):
    f = lambda a: np.ascontiguousarray(np.asarray(a, dtype=np.float32))
    x_prompt, x_sample, cache_k, cache_v = f(x_prompt), f(x_sample), f(cache_k), f(cache_v)
    state_conf, state_sconv = f(state_conf), f(state_sconv)
    nc, wspecs = _get_program()
    w_branch = f(w_branch)
    mats = {"w_in": f(w_in), "wb0": w_branch[:, 0], "wb1": w_branch[:, 1], "wb2": w_branch[:, 2],
            "w_out": f(w_out), "w_ffn_in": f(w_ffn_in), "w_ffn_out": f(w_ffn_out),
            "dww": f(conf_dw_w), "scw": f(sconv_w)}
    wts = _pack_weights(wspecs, mats)
    cstb = np.zeros((128, C_SZ), np.float32)
    cstb[:, C_ID:C_ID + 128] = np.eye(128, dtype=np.float32)
    p = np.arange(128)
    for l in range(2):
        o = C_PRM + l * P_SZ
        cstb[:, o + P_G1:o + P_G1 + 8] = f(norm1)[l].reshape(8, 128).T
        cstb[:, o + P_G2:o + P_G2 + 8] = f(norm2)[l].reshape(8, 128).T
        dww = f(conf_dw_w)[l]
        cstb[:, o + P_DWW:o + P_DWW + 124] = dww.reshape(31, 4, 128).transpose(2, 1, 0).reshape(128, 124)
        cstb[:, o + P_DWB:o + P_DWB + 4] = f(conf_dw_b)[l].reshape(4, 128).T
        cstb[:, o + P_LNG:o + P_LNG + 4] = f(conf_ln_g)[l].reshape(4, 128).T
        cstb[:, o + P_LNB:o + P_LNB + 4] = f(conf_ln_b)[l].reshape(4, 128).T
        scw = f(sconv_w)[l]
        cstb[:, o + P_SCW:o + P_SCW + 12] = scw.reshape(3, 4, 128).transpose(2, 1, 0).reshape(128, 12)
        sk = f(sinks)[l]
        for c in range(4):
            cstb[:, o + P_SINK + c] = np.where(p < 64, sk[c], sk[4 + c])
    cstb[:, C_GF:C_GF + 8] = f(final_norm).reshape(8, 128).T

    in_maps = []
    for core in range(8):
        seq, hf = core // 2, core % 2
        start = hf * 4096
        xs = x_sample[core * 16:(core + 1) * 16].reshape(128, D)
        if hf == 0:
            halo = np.zeros((256, D), np.float32)
        else:
            halo = x_prompt[seq, start - 256:start]
        xin = np.concatenate([xs, halo, x_prompt[seq, start:start + 4096]], axis=0)
        pos = np.concatenate([16384 + (np.arange(128) % 8), np.maximum(start - 256 + np.arange(256), 0),
                              start + np.arange(4096)])
        cc = cstb.copy()
        cc[:, C_FLAG] = float(hf)
        in_maps.append({
            "xin": np.ascontiguousarray(xin),
            "wts": wts,
            "cst": cc,
            "msk": _masks(hf == 0),
            "rope": _rope_tables(pos),
            "ck": np.ascontiguousarray(cache_k[:, core * 16:(core + 1) * 16].reshape(2, 16, 128, 128)),
            "cv": np.ascontiguousarray(cache_v[:, core * 16:(core + 1) * 16].reshape(2, 16, 128, 128)),
            "stc": np.ascontiguousarray(state_conf[:, core * 16:(core + 1) * 16].reshape(2, 480, 512)),
            "sts": np.ascontiguousarray(state_sconv[:, core * 16:(core + 1) * 16].reshape(2, 32, 512)),
        })
    ncore = int(os.environ.get('DBG_CORES', '8'))
    if ncore < 8:
        nu = int(os.environ.get('DBG_NUNIT', '158'))
        for m in in_maps:
            m['wts'] = m['wts'][:nu]
    res = run_bass_kernel_spmd(nc, in_maps[:ncore], core_ids=list(range(ncore)))
    R = list(res.results) + [res.results[0]] * (8 - ncore)
    y_prompt = np.empty((4, 8192, D), np.float32)
    y_sample = np.empty((128, 8, D), np.float32)
    k_prompt = np.empty((2, 4, 128, 2, 64), np.float32)
    v_prompt = np.empty((2, 4, 128, 2, 64), np.float32)
    conf_prompt = np.empty((2, 4, 30, 512), np.float32)
    sconv_prompt = np.empty((2, 4, 2, 512), np.float32)
    k_sample = np.empty((2, 128, 128, 2, 64), np.float32)
    v_sample = np.empty((2, 128, 128, 2, 64), np.float32)
    conf_sample = np.empty((2, 128, 30, 512), np.float32)
    sconv_sample = np.empty((2, 128, 2, 512), np.float32)
    for core in range(8):
        seq, hf = core // 2, core % 2
        r = R[core]
        yo = r["yout"]
        y_sample[core * 16:(core + 1) * 16] = yo[0:128].reshape(16, 8, D)
        y_prompt[seq, hf * 4096:(hf + 1) * 4096] = yo[384:]
        if hf == 1:
            k_prompt[:, seq] = r["kp"].reshape(2, 128, 2, 64)
            v_prompt[:, seq] = r["vp"].reshape(2, 128, 2, 64)
            conf_prompt[:, seq] = r["cp"]
            sconv_prompt[:, seq] = r["sp"]
        k_sample[:, core * 16:(core + 1) * 16] = r["ks"].reshape(2, 16, 128, 2, 64)
        v_sample[:, core * 16:(core + 1) * 16] = r["vs"].reshape(2, 16, 128, 2, 64)
        conf_sample[:, core * 16:(core + 1) * 16] = r["cs"]
        sconv_sample[:, core * 16:(core + 1) * 16] = r["ss"]
    return (y_prompt, y_sample, k_prompt, v_prompt, conf_prompt, sconv_prompt,
            k_sample, v_sample, conf_sample, sconv_sample)
```
